# Optimizing a Trainium2 kernel written in Bass

```python
import math
import jax, jax.numpy as jnp
from jax import lax
import numpy as np

D_MODEL = 1024
BATCH = 16
SEQ = 2048
DEPTH = 1

ATTN_HEADS = 8
ATTN_HEAD_DIM = 64
ATTN_V_DIM = 2 * ATTN_HEAD_DIM
ATTN_WIDTH = ATTN_HEADS * ATTN_V_DIM
Q_BLOCK = 128
CHUNK = 128
GMLP_GROUPS = 8
GMLP_GROUP_WIDTH = 128
GMLP_WIDTH = GMLP_GROUPS * GMLP_GROUP_WIDTH
REL_BUCKETS = 32
REL_MAX_EXACT = REL_BUCKETS // 2
REL_MAX_DIST = 128
FFN_HIDDEN = int(math.ceil((8 * D_MODEL / 3) / 256) * 256)
QK_WIDTH = ATTN_HEADS * 2 * ATTN_HEAD_DIM
IN_SPLITS = np.cumsum([QK_WIDTH, QK_WIDTH, ATTN_WIDTH, GMLP_WIDTH, GMLP_WIDTH, D_MODEL]).tolist()
IN_WIDTH = 2 * QK_WIDTH + ATTN_WIDTH + 2 * GMLP_WIDTH + 2 * D_MODEL
RMS_EPS = 1e-6
LN_EPS = 1e-5

kernel_name = "hybrid_diffattn_gmlp_gated_block"


def rmsnorm(x, g, eps=RMS_EPS):
    xf = x.astype(jnp.float32)
    y = xf * lax.rsqrt(jnp.mean(xf * xf, axis=-1, keepdims=True) + eps)
    return (y * g.astype(jnp.float32)).astype(x.dtype)


def layernorm(x, g, b, eps=LN_EPS):
    xf = x.astype(jnp.float32)
    mu = jnp.mean(xf, axis=-1, keepdims=True)
    xc = xf - mu
    y = xc * lax.rsqrt(jnp.mean(xc * xc, axis=-1, keepdims=True) + eps)
    return (y * g.astype(jnp.float32) + b.astype(jnp.float32)).astype(x.dtype)


def rel_bucket(dist):
    n = jnp.maximum(dist, 0)
    is_small = n < REL_MAX_EXACT
    nf = jnp.maximum(n, 1).astype(jnp.float32)
    large = REL_MAX_EXACT + (jnp.log(nf / REL_MAX_EXACT) / math.log(REL_MAX_DIST / REL_MAX_EXACT)
                             * (REL_BUCKETS - REL_MAX_EXACT)).astype(jnp.int32)
    large = jnp.minimum(large, REL_BUCKETS - 1)
    return jnp.where(is_small, n, large)


def diff_attention(q, k, v, lam, rel_bias):
    B, S, H, _, dh = q.shape
    nblk = S // Q_BLOCK
    scale = 1.0 / math.sqrt(dh)
    q_blocks = q.reshape(B, nblk, Q_BLOCK, H, 2, dh).transpose(1, 0, 2, 3, 4, 5)
    k_pos = jnp.arange(S, dtype=jnp.int32)

    def block(args):
        qb, bi = args
        s = jnp.einsum('bqhmd,bkhmd->bhmqk', qb, k,
                       preferred_element_type=jnp.float32) * scale
        q_pos = bi * Q_BLOCK + jnp.arange(Q_BLOCK, dtype=jnp.int32)
        dist = q_pos[:, None] - k_pos[None, :]
        bias = jnp.take(rel_bias.astype(jnp.float32), rel_bucket(dist), axis=0)
        bias = bias.transpose(2, 0, 1)[None, :, None]
        s = jnp.where((dist >= 0)[None, None, None], s + bias, -jnp.inf)
        p = jax.nn.softmax(s, axis=-1)
        a = (p[:, :, 0] - lam * p[:, :, 1]).astype(v.dtype)
        return jnp.einsum('bhqk,bkhe->bqhe', a, v)

    out = lax.map(block, (q_blocks, jnp.arange(nblk, dtype=jnp.int32)))
    return out.transpose(1, 0, 2, 3, 4).reshape(B, S, H, 2 * dh)


def chunked_spatial_gating(u, v, ln_g, ln_b, w_s, b_s):
    B, S, _ = u.shape
    nc = S // CHUNK
    v = layernorm(v, ln_g, ln_b)
    ws = w_s * jnp.tril(jnp.ones((CHUNK, CHUNK), dtype=w_s.dtype))
    vr = v.reshape(B, nc, CHUNK, GMLP_GROUPS, GMLP_GROUP_WIDTH)
    sv = jnp.einsum('gts,bnsgc->bntgc', ws, vr) + b_s.T[None, None, :, :, None]
    return (u.reshape(B, nc, CHUNK, GMLP_GROUPS, GMLP_GROUP_WIDTH) * sv).reshape(B, S, GMLP_WIDTH)


def setup_inputs(seed: int = 0) -> dict:
    key = jax.random.key(seed)
    ks = jax.random.split(key, 24)
    nrm = lambda k, shape, s: jax.random.normal(k, shape, dtype=jnp.float32) * s
    L, D = DEPTH, D_MODEL
    return {
        "x": nrm(ks[0], (BATCH, SEQ, D), 1.0),
        "c": nrm(ks[1], (BATCH, D), 1.0),
        "w_ada": nrm(ks[2], (L, D, 6 * D), D ** -0.5),
        "b_ada": nrm(ks[3], (L, 6 * D), 0.02),
        "norm1_g": 1.0 + nrm(ks[4], (L, D), 0.02),
        "norm2_g": 1.0 + nrm(ks[5], (L, D), 0.02),
        "w_in": nrm(ks[6], (L, D, IN_WIDTH), D ** -0.5),
        "lambda_q1": nrm(ks[7], (L, ATTN_HEAD_DIM), 0.1),
        "lambda_k1": nrm(ks[8], (L, ATTN_HEAD_DIM), 0.1),
        "lambda_q2": nrm(ks[9], (L, ATTN_HEAD_DIM), 0.1),
        "lambda_k2": nrm(ks[10], (L, ATTN_HEAD_DIM), 0.1),
        "subln_g": 1.0 + nrm(ks[11], (L, ATTN_V_DIM), 0.02),
        "ln_v_g": 1.0 + nrm(ks[12], (L, GMLP_WIDTH), 0.02),
        "ln_v_b": nrm(ks[13], (L, GMLP_WIDTH), 0.02),
        "w_spatial": nrm(ks[14], (L, GMLP_GROUPS, CHUNK, CHUNK), CHUNK ** -0.5),
        "b_spatial": 1.0 + nrm(ks[15], (L, GMLP_GROUPS, CHUNK), 0.1),
        "w_proj_a": nrm(ks[16], (L, ATTN_WIDTH, D), ATTN_WIDTH ** -0.5),
        "w_proj_b": nrm(ks[17], (L, GMLP_WIDTH, D), GMLP_WIDTH ** -0.5),
        "w_out": nrm(ks[18], (L, D, D), D ** -0.5),
        "w_ffn_in": nrm(ks[19], (L, D, 2 * FFN_HIDDEN), D ** -0.5),
        "w_ffn_out": nrm(ks[20], (L, FFN_HIDDEN, D), FFN_HIDDEN ** -0.5),
        "rel_bias": nrm(ks[21], (REL_BUCKETS, ATTN_HEADS), 0.5),
        "final_g": 1.0 + nrm(ks[22], (D,), 0.02),
    }


def reference(x, c, w_ada, b_ada, norm1_g, norm2_g, w_in, lambda_q1, lambda_k1, lambda_q2,
              lambda_k2, subln_g, ln_v_g, ln_v_b, w_spatial, b_spatial, w_proj_a, w_proj_b,
              w_out, w_ffn_in, w_ffn_out, rel_bias, final_g):
    B, S, D = x.shape
    c_act = jax.nn.silu(c)
    for l in range(DEPTH):
        mod = c_act @ w_ada[l] + b_ada[l]
        sh1, sc1, g1, sh2, sc2, g2 = [m[:, None, :] for m in jnp.split(mod, 6, axis=-1)]

        h = rmsnorm(x, norm1_g[l]) * (1.0 + sc1) + sh1
        z = h @ w_in[l]
        zq, zk, zv, zu, zg, ga, gb = jnp.split(z, IN_SPLITS, axis=-1)

        lam_init = 0.8 - 0.6 * math.exp(-0.3 * l)
        lam = (jnp.exp(jnp.sum(lambda_q1[l] * lambda_k1[l]).astype(jnp.float32))
               - jnp.exp(jnp.sum(lambda_q2[l] * lambda_k2[l]).astype(jnp.float32)) + lam_init)
        q = zq.reshape(B, S, ATTN_HEADS, 2, ATTN_HEAD_DIM)
        k = zk.reshape(B, S, ATTN_HEADS, 2, ATTN_HEAD_DIM)
        v = zv.reshape(B, S, ATTN_HEADS, ATTN_V_DIM)
        oa = diff_attention(q, k, v, lam, rel_bias)
        oa = (rmsnorm(oa, subln_g[l]) * (1.0 - lam_init)).reshape(B, S, ATTN_WIDTH)

        ob = chunked_spatial_gating(jax.nn.gelu(zu), jax.nn.gelu(zg), ln_v_g[l], ln_v_b[l],
                                    w_spatial[l], b_spatial[l])

        merged = jax.nn.sigmoid(ga) * (oa @ w_proj_a[l]) + jax.nn.sigmoid(gb) * (ob @ w_proj_b[l])
        x = x + g1 * (merged @ w_out[l])

        h2 = rmsnorm(x, norm2_g[l]) * (1.0 + sc2) + sh2
        gate, up = jnp.split(h2 @ w_ffn_in[l], 2, axis=-1)
        x = x + g2 * ((jax.nn.silu(gate) * up) @ w_ffn_out[l])

    return rmsnorm(x, final_g)
```

```python
import math
from contextlib import ExitStack

import numpy as np
import concourse.bass as bass
import concourse.mybir as mybir
from concourse.bass_utils import run_bass_kernel_spmd

F32 = mybir.dt.float32
BF16 = mybir.dt.bfloat16
AF = mybir.ActivationFunctionType
ALU = mybir.AluOpType
AX = mybir.AxisListType

NCORES = 8
NB = 2
S = 2048
D = 1024
H = 8
FF = 2816
RMS_EPS = 1e-6
LN_EPS = 1e-5
LAM_INIT = 0.2
NEG = -30000.0
ENGS = ["pe", "act", "dve", "pool", "sp"]


class Op:
    __slots__ = ("eng", "fn", "deps", "marked", "seq", "dma", "dma_val", "pos")

    def __init__(self, eng, fn, dma):
        self.eng, self.fn, self.dma = eng, fn, dma
        self.deps, self.marked, self.seq, self.dma_val = [], False, 0, 0


class Trk:
    def __init__(self, name, nbytes):
        self.name = name
        self.segs = [[0, nbytes, None, {}]]

    def access(self, lo, hi, op, write):
        deps, new = [], []
        for seg in self.segs:
            s_lo, s_hi, w, rs = seg
            if s_hi <= lo or s_lo >= hi:
                new.append(seg)
                continue
            if s_lo < lo:
                new.append([s_lo, lo, w, dict(rs)])
            m_lo, m_hi = max(s_lo, lo), min(s_hi, hi)
            if write:
                if w is not None:
                    deps.append(("waw", w))
                deps += [("war", r) for r in rs.values()]
                new.append([m_lo, m_hi, op, {}])
            else:
                if w is not None:
                    deps.append(("raw", w))
                rs2 = dict(rs)
                rs2[op.dma if op.dma is not None else op.eng] = op
                new.append([m_lo, m_hi, w, rs2])
            if s_hi > hi:
                new.append([hi, s_hi, w, dict(rs)])
        merged = []
        for seg in new:
            if merged and merged[-1][1] == seg[0] and merged[-1][2] is seg[2] and merged[-1][3] == seg[3]:
                merged[-1][1] = seg[1]
            else:
                merged.append(seg)
        self.segs = merged
        return deps


class V:
    def __init__(self, ap, trk, base, esz, shape):
        self.ap, self.trk, self.base, self.esz, self.shape = ap, trk, base, esz, list(shape)
        st, acc = [], 1
        for n in reversed(self.shape):
            st.append(acc)
            acc *= n
        self.strides = list(reversed(st))
        self.total = acc

    def span(self, *idx):
        lo = hi = 0
        for d, n in enumerate(self.shape):
            ix = idx[d] if d < len(idx) else None
            if ix is None:
                a, b = 0, n
            elif isinstance(ix, tuple):
                a, b = ix
            else:
                a, b = ix, ix + 1
            lo += a * self.strides[d]
            hi += (b - 1) * self.strides[d]
        return (self.trk, self.base + lo * self.esz, self.base + (hi + 1) * self.esz)


class Prog:
    def __init__(self, nc):
        self.nc = nc
        self.ops = {e: [] for e in ENGS}
        self.dma_cnt = {}
        self.n_anon = 0
        self.final = []

    def op(self, eng, fn, reads=(), writes=(), dma=None, final=False):
        if dma == "new":
            self.n_anon += 1
            dma = "anon%d" % self.n_anon
        o = Op(eng, fn, dma)
        deps = []
        for (t, lo, hi) in reads:
            deps += t.access(lo, hi, o, False)
        for (t, lo, hi) in writes:
            deps += t.access(lo, hi, o, True)
        best = {}
        for kind, p in deps:
            if p is o:
                continue
            if p.dma is None and o.dma is None and p.eng == eng:
                if eng == "pe" or kind != "raw":
                    continue
            key = ("d", p.dma) if p.dma is not None else ("e", p.eng)
            rank = p.dma_val if p.dma is not None else p.pos
            if key not in best or rank > best[key][0]:
                best[key] = (rank, p)
        for _, p in best.values():
            o.deps.append(p)
            if p.dma is None:
                p.marked = True
        o.pos = len(self.ops[eng])
        if dma is not None:
            self.dma_cnt[dma] = self.dma_cnt.get(dma, 0) + 1
            o.dma_val = 16 * self.dma_cnt[dma]
        self.ops[eng].append(o)
        if final:
            self.final.append(o)
        return o

    def emit(self):
        nc = self.nc
        for e in ENGS:
            n = 0
            for o in self.ops[e]:
                if o.dma is None and o.marked:
                    n += 1
                    o.seq = n
        needed = set()
        for e in ENGS:
            seen = {}
            deps_iter = [p for o in self.ops[e] for p in o.deps]
            if e == "sp":
                deps_iter += list(self.final)
            for p in deps_iter:
                key, val = (("d", p.dma), p.dma_val) if p.dma is not None else (("e", p.eng), p.seq)
                if seen.get(key, 0) >= val:
                    continue
                seen[key] = val
                if p.dma is None:
                    needed.add(id(p))
        for e in ENGS:
            n = 0
            for o in self.ops[e]:
                if o.dma is None:
                    o.marked = id(o) in needed
                    o.seq = 0
                    if o.marked:
                        n += 1
                        o.seq = n
            self.n_marks = getattr(self, "n_marks", {})
            self.n_marks[e] = n
        with ExitStack() as es:
            LIMIT = 900
            esem = {e: [es.enter_context(nc.semaphore("sem_%s_%d" % (e, k)))
                        for k in range(self.n_marks[e] // LIMIT + 1)] for e in ENGS}
            dsem = {k: es.enter_context(nc.semaphore("dma_" + k)) for k in self.dma_cnt}
            block = es.enter_context(nc.Block())
            stats = {}

            def run(ename, eng):
                seen = {}
                nwait = 0
                ops = self.ops[ename]

                def wait_for(p):
                    nonlocal nwait
                    if p.dma is not None:
                        key, gval = ("d", p.dma), p.dma_val
                    else:
                        key, gval = ("e", p.eng), p.seq
                    if seen.get(key, 0) >= gval:
                        return
                    seen[key] = gval
                    if p.dma is not None:
                        eng.wait_ge(dsem[p.dma], gval)
                    else:
                        eng.wait_ge(esem[p.eng][(gval - 1) // LIMIT], (gval - 1) % LIMIT + 1)
                    nwait += 1

                for o in ops:
                    for p in o.deps:
                        wait_for(p)
                    inst = o.fn(eng)
                    if o.dma is not None:
                        inst.then_inc(dsem[o.dma], 16)
                    elif o.marked:
                        inst.then_inc(esem[ename][(o.seq - 1) // LIMIT], 1)
                if ename == "sp":
                    for o in self.final:
                        wait_for(o)
                stats[ename] = (len(ops), nwait)

            @block.tensor
            def _(e):
                run("pe", e)

            @block.scalar
            def _(e):
                run("act", e)

            @block.vector
            def _(e):
                run("dve", e)

            @block.gpsimd
            def _(e):
                run("pool", e)

            @block.sync
            def _(e):
                run("sp", e)

            self.stats = stats


def rel_bucket_ranges():
    d = np.arange(0, 256)
    nf = np.maximum(d, 1).astype(np.float32)
    large = 16 + (np.log(nf / np.float32(16)) / np.float32(math.log(8.0)) * np.float32(16)).astype(np.int32)
    large = np.minimum(large, 31)
    bk = np.where(d < 16, d, large)
    out = []
    for v in range(16, 32):
        idx = np.nonzero(bk == v)[0]
        if len(idx):
            assert idx[-1] - idx[0] + 1 == len(idx)
            out.append((v, int(idx[0]), int(idx[-1]) + 1))
    assert all(bk[113:] == 31)
    return out


def build(dbg_stage=None):
    nc = bass.Bass("TRN2", target_bir_lowering=False)
    P = Prog(nc)

    def din(name, shape, dt=F32):
        return nc.dram_tensor(name, list(shape), dt, kind="ExternalInput").ap()

    x_d = din("x", [NB, S, D])
    cT_d = din("cT", [128, 8, NB])
    wada_d = din("w_ada", [D, 6 * D])
    bada_d = din("b_ada", [1, 6 * D])
    badaT_d = din("b_adaT", [128, 48])
    n1gT_d = din("n1gT", [128, 8])
    n2gT_d = din("n2gT", [128, 8])
    win_d = din("w_in", [D, 7 * D])
    lam4_d = din("lam4", [1, 256])
    subg_d = din("subln_g", [1, 128])
    lng_d = din("ln_v_g", [1, D])
    lnb_d = din("ln_v_b", [1, D])
    wspT_d = din("w_spT", [8, 128, 128])
    bsp_d = din("b_sp", [1, D])
    wpa_d = din("w_proj_a", [D, D])
    wpb_d = din("w_proj_b", [D, D])
    wout_d = din("w_out", [D, D])
    wfi_d = din("w_ffn_in", [D, 2 * FF])
    wfo_d = din("w_ffn_out", [FF, D])
    rbT_d = din("rel_biasT", [8, 32])
    rb_d = din("rel_bias", [32, 8])
    gf_d = din("final_g", [1, D])
    y_d = nc.dram_tensor("y", [NB, S, D], F32, kind="ExternalOutput").ap()
    Fd = nc.dram_tensor("Fd_scratch", [8, 384], BF16, kind="Internal").ap()
    gsc = nc.dram_tensor("g_scratch", [2 * NB, D], F32, kind="Internal").ap()

    def bcast(ap2d, off, n):
        return bass.AP(ap2d.tensor, off, [[0, 128], [1, n]])

    win_v = win_d.rearrange("(k p) n -> p k n", p=128)
    wada_v = wada_d.rearrange("(k p) n -> p k n", p=128)
    wpa_v = wpa_d.rearrange("(k p) n -> p k n", p=128)
    wpb_v = wpb_d.rearrange("(k p) n -> p k n", p=128)
    wout_v = wout_d.rearrange("(k p) n -> p k n", p=128)
    wfi_v = wfi_d.rearrange("(k p) n -> p k n", p=128)
    wfo_v = wfo_d.rearrange("(k p) n -> p k n", p=128)

    es = ExitStack()
    with es:
        def sb(name, shape, dt):
            t = es.enter_context(nc.sbuf_tensor("sb_" + name, list(shape), dt))
            esz = 4 if dt == F32 else 2
            n = int(np.prod(shape[1:]))
            ap = t[tuple(slice(None) for _ in shape)]
            return V(ap, Trk(name, n * esz), 0, esz, shape[1:])

        ident_f = sb("ident_f", [128, 128], F32)
        ones_f = sb("ones_f", [128, 128], F32)
        ident = sb("ident", [128, 128], BF16)
        Jb = sb("Jb", [128, 128], BF16)
        wsT = sb("wsT", [128, 8, 128], BF16)
        BT = sb("BT", [128, 8, 256], BF16)
        b31bc = sb("b31bc", [128, 8], F32)
        nlam = sb("nlam", [128, 1], F32)
        gsub = sb("gsub", [128, 128], F32)
        modT = sb("modT", [128, 48, NB], F32)
        gm1 = sb("gm1", [128, 8, NB], F32)
        gm2 = sb("gm2", [128, 8, NB], F32)
        n1gT = sb("n1gT", [128, 8], F32)
        n2gT = sb("n2gT", [128, 8], F32)
        small = sb("small", [128, 64], F32)
        bc = [sb("bc%d" % i, [128, D], F32) for i in range(6)]
        hT = sb("hT", [128, 8, S], BF16)
        OAT = sb("OAT", [128, 8, S], BF16)
        NSLOT = 4
        wslot = [sb("wslot%d" % i, [128, 8, 512], BF16) for i in range(NSLOT)]

        ARENA_E = 37120
        arena_t = es.enter_context(nc.sbuf_tensor("arena", [128, ARENA_E], BF16))
        arena_trk = Trk("arena", ARENA_E * 2)

        def carve(off_b, shape, dt):
            esz = 4 if dt == F32 else 2
            n = int(np.prod(shape))
            assert off_b % 4 == 0 and off_b + n * esz <= ARENA_E * 2, (off_b, shape)
            ap = arena_t[:, off_b // 2: off_b // 2 + n * esz // 2]
            if dt == F32:
                ap = ap.bitcast(F32)
            if len(shape) == 2:
                ap = ap.rearrange("p (a b) -> p a b", b=shape[1])
            elif len(shape) == 3:
                ap = ap.rearrange("p (a b c) -> p a b c", b=shape[1], c=shape[2])
            elif len(shape) == 4:
                ap = ap.rearrange("p (a b c d) -> p a b c d", b=shape[1], c=shape[2], d=shape[3])
            return V(ap, arena_trk, off_b, esz, shape)

        banks = []
        for i in range(8):
            t = es.enter_context(nc.psum_tensor("bank%d" % i, [128, 512], F32))
            banks.append(V(t[:, :], Trk("bank%d" % i, 2048), 0, 4, [512]))

        rr = {"bank": 0, "slot": 0}

        def next_bank(nb=7):
            i = rr["bank"] % nb
            rr["bank"] += 1
            return banks[i]

        def load_w(src, kc, n):
            i = rr["slot"] % NSLOT
            rr["slot"] += 1
            sl = wslot[i]
            P.op("pool", lambda e: e.dma_start(out=sl.ap[:, 0:kc, 0:n], in_=src),
                 writes=[sl.span()], dma="w%d" % i)
            return sl

        def mm(ps_ap, lhsT, rhs, start, stop, reads, writes, **kw):
            P.op("pe", lambda e: e.matmul(ps_ap, lhsT, rhs, start=start, stop=stop, **kw),
                 reads=reads, writes=writes)

        def rstd_from(ss_ap, ss_span, out_ap, out_span, n, eps):
            P.op("act", lambda e: e.activation(out=out_ap, in_=ss_ap, func=AF.Ln, bias=epsc[eps], scale=1.0 / n),
                 reads=[ss_span, small.span((60, 62))], writes=[out_span])
            P.op("act", lambda e: e.activation(out=out_ap, in_=out_ap, func=AF.Exp, scale=-0.5),
                 reads=[out_span], writes=[out_span])

        dbg = []

        def dump(name, view, shape, dt, span):
            d = nc.dram_tensor("dbg_" + name, list(shape), dt, kind="ExternalOutput").ap()
            P.op("sp", lambda e: e.dma_start(out=d, in_=view), reads=[span], dma="new", final=True)
            dbg.append("dbg_" + name)

        P.op("pool", lambda e: e.memset(ones_f.ap, 1.0), writes=[ones_f.span()])
        P.op("pool", lambda e: e.memset(small.ap[:, 60:61], RMS_EPS), writes=[small.span((60, 61))])
        P.op("pool", lambda e: e.memset(small.ap[:, 61:62], LN_EPS), writes=[small.span((61, 62))])
        epsc = {RMS_EPS: small.ap[:, 60:61], LN_EPS: small.ap[:, 61:62]}
        P.op("pool", lambda e: e.affine_select(out=ident_f.ap, in_=ones_f.ap, pattern=[[-1, 128]],
                                               compare_op=ALU.is_equal, fill=0.0, base=0, channel_multiplier=1),
             reads=[ones_f.span()], writes=[ident_f.span()])
        P.op("dve", lambda e: e.tensor_copy(out=ident.ap, in_=ident_f.ap), reads=[ident_f.span()],
             writes=[ident.span()])
        jf = carve(0, [128], F32)
        P.op("pool", lambda e: e.affine_select(out=jf.ap, in_=ones_f.ap, pattern=[[1, 128]],
                                               compare_op=ALU.is_equal, fill=0.0, base=-127, channel_multiplier=1),
             reads=[ones_f.span()], writes=[jf.span()])
        P.op("dve", lambda e: e.tensor_copy(out=Jb.ap, in_=jf.ap), reads=[jf.span()], writes=[Jb.span()])

        wst = carve(512, [8, 128], F32)
        P.op("sp", lambda e: e.dma_start(out=wst.ap, in_=wspT_d.rearrange("g s t -> s g t")),
             writes=[wst.span()], dma="new")
        P.op("pool", lambda e: e.affine_select(out=wsT.ap, in_=wst.ap, pattern=[[0, 8], [1, 128]],
                                               compare_op=ALU.is_ge, fill=0.0, base=0, channel_multiplier=-1),
             reads=[wst.span()], writes=[wsT.span()])

        if dbg_stage == "c1":
            dump("wsT", wsT.ap, [128, 8, 128], BF16, wsT.span())
            dump("ident", ident.ap, [128, 128], BF16, ident.span())
            dump("Jb", Jb.ap, [128, 128], BF16, Jb.span())
            P.emit()
            return nc, P, dbg
        RB = carve(4608, [32], F32)
        Ft = carve(4736, [384], F32)
        Fb = carve(6272, [384], BF16)
        P.op("sp", lambda e: e.dma_start(out=RB.ap[0:8, :], in_=rbT_d), writes=[RB.span()], dma="new")
        P.op("dve", lambda e: e.memset(Ft.ap[0:8, :], 0.0), writes=[Ft.span()])
        P.op("dve", lambda e: e.tensor_copy(out=Ft.ap[0:8, 127:143], in_=RB.ap[0:8, 0:16]),
             reads=[RB.span()], writes=[Ft.span()])
        for (v, lo, hi) in rel_bucket_ranges():
            P.op("dve", lambda e, v=v, lo=lo, hi=hi: e.tensor_scalar(
                out=Ft.ap[0:8, 127 + lo:127 + hi], in0=Ft.ap[0:8, 127 + lo:127 + hi],
                scalar1=RB.ap[0:8, v:v + 1], scalar2=None, op0=ALU.add),
                reads=[RB.span(), Ft.span()], writes=[Ft.span()])
        P.op("dve", lambda e: e.tensor_scalar(out=Ft.ap[0:8, 127:384], in0=Ft.ap[0:8, 127:384],
                                              scalar1=RB.ap[0:8, 31:32], scalar2=8.0,
                                              op0=ALU.subtract, op1=ALU.mult),
             reads=[RB.span(), Ft.span()], writes=[Ft.span()])
        P.op("dve", lambda e: e.memset(Ft.ap[0:8, 0:127], NEG), reads=[Ft.span()], writes=[Ft.span()])
        P.op("dve", lambda e: e.tensor_copy(out=Fb.ap[0:8, :], in_=Ft.ap[0:8, :]), reads=[Ft.span()],
             writes=[Fb.span()])
        fd_trk = Trk("Fd", 8 * 384 * 2)
        P.op("sp", lambda e: e.dma_start(out=Fd, in_=Fb.ap[0:8, :]), reads=[Fb.span()],
             writes=[(fd_trk, 0, 8 * 384 * 2)], dma="new")
        P.op("sp", lambda e: e.dma_start(out=BT.ap, in_=bass.AP(Fd.tensor, 0, [[1, 128], [384, 8], [1, 256]])),
             reads=[(fd_trk, 0, 8 * 384 * 2)], writes=[BT.span()], dma="new")
        P.op("sp", lambda e: e.dma_start(out=b31bc.ap, in_=bcast(rb_d, 31 * 8, 8)), writes=[b31bc.span()],
             dma="new")

        if dbg_stage == "c2":
            dump("BT", BT.ap, [128, 8, 256], BF16, BT.span())
            P.emit()
            return nc, P, dbg
        lamt = carve(7168, [256], F32)
        lprod = carve(8192, [2, 64], F32)
        P.op("sp", lambda e: e.dma_start(out=lamt.ap, in_=bcast(lam4_d, 0, 256)), writes=[lamt.span()], dma="new")
        for i in range(2):
            P.op("dve", lambda e, i=i: e.tensor_tensor(out=lprod.ap[:, i, :], in0=lamt.ap[:, 128 * i:128 * i + 64],
                                                       in1=lamt.ap[:, 128 * i + 64:128 * i + 128], op=ALU.mult),
                 reads=[lamt.span()], writes=[lprod.span(i)])
            P.op("dve", lambda e, i=i: e.reduce_sum(out=small.ap[:, i:i + 1], in_=lprod.ap[:, i, :], axis=AX.X),
                 reads=[lprod.span(i)], writes=[small.span((i, i + 1))])
        P.op("act", lambda e: e.activation(out=small.ap[:, 2:4], in_=small.ap[:, 0:2], func=AF.Exp),
             reads=[small.span((0, 2))], writes=[small.span((2, 4))])
        P.op("dve", lambda e: e.tensor_tensor(out=nlam.ap, in0=small.ap[:, 3:4], in1=small.ap[:, 2:3],
                                              op=ALU.subtract),
             reads=[small.span((2, 4))], writes=[nlam.span()])
        P.op("dve", lambda e: e.tensor_scalar(out=nlam.ap, in0=nlam.ap, scalar1=-LAM_INIT, scalar2=None,
                                              op0=ALU.add),
             reads=[nlam.span()], writes=[nlam.span()])
        P.op("sp", lambda e: e.dma_start(out=gsub.ap, in_=bcast(subg_d, 0, 128)), writes=[gsub.span()], dma="new")
        P.op("dve", lambda e: e.tensor_scalar(out=gsub.ap, in0=gsub.ap, scalar1=1.0 - LAM_INIT, scalar2=None,
                                              op0=ALU.mult),
             reads=[gsub.span()], writes=[gsub.span()])

        if dbg_stage == "c3":
            dump("nlam", nlam.ap, [128, 1], F32, nlam.span())
            dump("gsub", gsub.ap, [128, 128], F32, gsub.span())
            P.emit()
            return nc, P, dbg
        cTt = carve(8704, [8, NB], F32)
        cact = carve(8768, [8, NB], F32)
        cactb = carve(8832, [8, NB], BF16)
        badaT = carve(8896, [48], F32)
        crep = [carve(9216 + 2048 * b, [8, 128], BF16) for b in range(NB)]
        babc = [carve(13312 + 2048 * i, [512], F32) for i in range(2)]
        gtmp = carve(17408, [512], F32)
        P.op("sp", lambda e: e.dma_start(out=cTt.ap, in_=cT_d), writes=[cTt.span()], dma="new")
        P.op("sp", lambda e: e.dma_start(out=badaT.ap, in_=badaT_d), writes=[badaT.span()], dma="new")
        P.op("sp", lambda e: e.dma_start(out=n1gT.ap, in_=n1gT_d), writes=[n1gT.span()], dma="new")
        P.op("sp", lambda e: e.dma_start(out=n2gT.ap, in_=n2gT_d), writes=[n2gT.span()], dma="new")
        P.op("act", lambda e: e.activation(out=cact.ap, in_=cTt.ap, func=AF.Silu), reads=[cTt.span()],
             writes=[cact.span()])
        P.op("dve", lambda e: e.tensor_copy(out=cactb.ap, in_=cact.ap), reads=[cact.span()], writes=[cactb.span()])
        for b in range(NB):
            for kc in range(8):
                P.op("dve", lambda e, b=b, kc=kc: e.tensor_scalar(
                    out=crep[b].ap[:, kc, :], in0=ones_f.ap, scalar1=cact.ap[:, kc, b:b + 1], scalar2=None,
                    op0=ALU.mult),
                    reads=[ones_f.span(), cact.span()], writes=[crep[b].span(kc)])
        modps = V(banks[6].ap[:, 0:96].rearrange("p (c b) -> p c b", b=NB), banks[6].trk, 0, 4, [48, NB])
        gtrk = Trk("gsc", 2 * NB * D * 4)
        for wt in range(12):
            sl = load_w(wada_v[:, :, 512 * wt:512 * wt + 512], 8, 512)
            for cc in range(4):
                c = 4 * wt + cc
                for kc in range(8):
                    mm(modps.ap[:, c, :], sl.ap[:, kc, 128 * cc:128 * cc + 128], cactb.ap[:, kc, :],
                       kc == 0, kc == 7, [sl.span(), cactb.span()], [modps.span(c)])
            if False:
                which = 0 if wt < 6 else 1
                half = wt % 2
                bb = babc[half]
                P.op("sp", lambda e, bb=bb, wt=wt: e.dma_start(out=bb.ap, in_=bcast(bada_d, 512 * wt, 512)),
                     writes=[bb.span()], dma="bb%d" % half)
                for b in range(NB):
                    ps = next_bank()
                    for kc in range(8):
                        mm(ps.ap, crep[b].ap[:, kc, :], sl.ap[:, kc, :], kc == 0, kc == 7,
                           [sl.span(), crep[b].span(kc)], [ps.span()])
                    P.op("dve", lambda e, ps=ps, bb=bb: e.tensor_tensor(out=gtmp.ap, in0=ps.ap, in1=bb.ap,
                                                                        op=ALU.add),
                         reads=[ps.span(), bb.span()], writes=[gtmp.span()])
                    row = 2 * b + which
                    off = (row * D + 512 * half) * 4
                    P.op("sp", lambda e, row=row, half=half: e.dma_start(
                        out=gsc[row:row + 1, 512 * half:512 * half + 512], in_=gtmp.ap[0:1, :]),
                        reads=[gtmp.span()], writes=[(gtrk, off, off + 2048)], dma="gsc")
        for b in range(NB):
            P.op("dve", lambda e, b=b: e.tensor_tensor(out=modT.ap[:, :, b], in0=modps.ap[:, :, b], in1=badaT.ap,
                                                       op=ALU.add),
                 reads=[modps.span(), badaT.span()], writes=[modT.span()])
        for b in range(NB):
            P.op("dve", lambda e, b=b: e.scalar_tensor_tensor(out=gm1.ap[:, :, b], in0=modT.ap[:, 8:16, b],
                                                              scalar=1.0, in1=n1gT.ap, op0=ALU.add, op1=ALU.mult),
                 reads=[modT.span(), n1gT.span()], writes=[gm1.span()])
            P.op("dve", lambda e, b=b: e.scalar_tensor_tensor(out=gm2.ap[:, :, b], in0=modT.ap[:, 32:40, b],
                                                              scalar=1.0, in1=n2gT.ap, op0=ALU.add, op1=ALU.mult),
                 reads=[modT.span(), n2gT.span()], writes=[gm2.span()])

        if dbg_stage in ("c4", "c4a"):
            dump("modT", modT.ap, [128, 48, NB], F32, modT.span())
            dump("gm1", gm1.ap, [128, 8, NB], F32, gm1.span())
            P.emit()
            return nc, P, dbg
        def next_tp():
            return banks[7]

        def norm_to_T(src_ap_of_tb, src_span_of_tb, dstT, dst_t0, gm, sh_lo, b, scr_off):
            junk = carve(scr_off, [D], BF16)
            xs = carve(scr_off + 2048, [4, D], BF16)
            for tb in range(4):
                sap, ssp = src_ap_of_tb(tb), src_span_of_tb(tb)
                c0 = 8 + tb
                P.op("act", lambda e, sap=sap, c0=c0: e.activation(out=junk.ap, in_=sap, func=AF.Square,
                                                                   accum_out=small.ap[:, c0:c0 + 1]),
                     reads=[ssp], writes=[junk.span(), small.span((c0, c0 + 1))])
            rstd_from(small.ap[:, 8:12], small.span((8, 12)), small.ap[:, 12:16], small.span((12, 16)), D, RMS_EPS)
            for tb in range(4):
                sap, ssp = src_ap_of_tb(tb), src_span_of_tb(tb)
                c0 = 8 + tb
                P.op("dve", lambda e, sap=sap, tb=tb, c0=c0: e.tensor_scalar(
                    out=xs.ap[:, tb, :], in0=sap, scalar1=small.ap[:, c0 + 4:c0 + 5], scalar2=None, op0=ALU.mult),
                    reads=[ssp, small.span((c0 + 4, c0 + 5))], writes=[xs.span(tb)])
            if dbg_stage == "a1":
                dump("xs", xs.ap, [128, 4, D], BF16, xs.span())
                dump("small", small.ap, [128, 64], F32, small.span())
                raise StopIteration
            for kc in range(8):
                tp = next_bank()
                for tb in range(4):
                    mm(tp.ap[:, tb * 128:tb * 128 + 128], xs.ap[:, tb, kc * 128:kc * 128 + 128], ident.ap, True, True,
                       [xs.span(tb), ident.span()], [tp.span((tb * 128, tb * 128 + 128))])
                P.op("dve", lambda e, tp=tp, kc=kc: e.tensor_scalar(
                    out=dstT.ap[:, kc, dst_t0:dst_t0 + 512], in0=tp.ap, scalar1=gm.ap[:, kc, b:b + 1],
                    scalar2=modT.ap[:, sh_lo + kc, b:b + 1], op0=ALU.mult, op1=ALU.add),
                    reads=[tp.span(), gm.span(), modT.span()], writes=[dstT.span(kc, (dst_t0, dst_t0 + 512))])

        A_V1 = 0
        A_QK = 33280
        A_WQK = 49664
        A_PT = 57856
        A_FIN = 64000
        V1 = carve(A_V1, [16, 8, 129], BF16)
        QK = [carve(A_QK + 8192 * i, [2, S], BF16) for i in range(2)]
        WQK = [carve(A_WQK + 4096 * i, [2, 8, 128], BF16) for i in range(2)]
        PT = [[carve(A_PT + 2048 * i + 1024 * m, [512], BF16) for m in range(2)] for i in range(3)]
        FT1 = [carve(A_FIN + 512 * i, [128], F32) for i in range(2)]
        FO = [carve(A_FIN + 1024 + 512 * i, [128], F32) for i in range(4)]
        FJ = [carve(A_FIN + 3072 + 256 * i, [128], BF16) for i in range(2)]
        OAB = [carve(A_FIN + 3584 + 1024 * i, [4, 128], BF16) for i in range(2)]
        ST = [[banks[0], banks[1]], [banks[2], banks[3]]]
        ACCS = [carve(69632, [387], F32), carve(69632 + 1548, [387], F32), carve(69632 + 3096, [258], F32)]
        tokST = [Trk("tokST%d" % i, 4) for i in range(2)]
        tokPT = [Trk("tokPT%d" % i, 4) for i in range(3)]
        tokACC = Trk("tokACC", 4)
        for i in (0, 1, 2, 3):
            pass

        def acc(m, i):
            if i < 3:
                bk = banks[4 + m]
                return bk, bk.ap[:, 129 * i:129 * i + 129], (129 * i, 129 * i + 129)
            bk = banks[6]
            return bk, bk.ap[:, 129 * m:129 * m + 129], (129 * m, 129 * m + 129)

        pending = []

        def tick_pending():
            for it in pending:
                it[0] -= 1
            while pending and pending[0][0] <= 0:
                pending.pop(0)[1]()

        def flush_pending():
            while pending:
                pending.pop(0)[1]()

        def attention(b, h, s, inj=()):
            inj = list(inj)
            stepc = [0]
            QT = lambda m, a, bnd: QK[s].ap[64 * m:64 * m + 64, 0, a:bnd]
            KT = lambda m, a, bnd: QK[s].ap[64 * m:64 * m + 64, 1, a:bnd]
            for I in range(4):
                last = 4 * I + 3
                started = set()

                def qk(j):
                    r = j - 4 * I
                    c0 = 128 * max(r, 0)
                    has_bias = j >= 4 * I - 1
                    stb = ST[j % 2]
                    pt = PT[j % 3]
                    for m in range(2):
                        mm(stb[m].ap[:, c0:512], KT(m, j * 128, j * 128 + 128), QT(m, 512 * I + c0, 512 * I + 512),
                           True, not has_bias,
                           [QK[s].span(1, (j * 128, j * 128 + 128)), QK[s].span(0, (512 * I + c0, 512 * I + 512))],
                           [stb[0].span(), stb[1].span(), (tokST[j % 2], 0, 4)])
                    if has_bias:
                        if r < 0:
                            cb, boff, w = 0, 128, 128
                        else:
                            cb, boff, w = c0, 0, min(256, 512 - c0)
                        for m in range(2):
                            mm(stb[m].ap[:, cb:cb + w], Jb.ap, BT.ap[:, h, boff:boff + w], False, True,
                               [Jb.span(), BT.span(h)], [stb[0].span(), stb[1].span(), (tokST[j % 2], 0, 4)],
                               skip_group_check=True)
                    for m in range(2):
                        P.op("act", lambda e, m=m, stb=stb, pt=pt, c0=c0: e.activation(
                            out=pt[m].ap[:, c0:512], in_=stb[m].ap[:, c0:512], func=AF.Exp,
                            bias=b31bc.ap[:, h:h + 1], scale=0.125),
                            reads=[(tokST[j % 2], 0, 4), stb[0].span(), stb[1].span(), b31bc.span()],
                            writes=[pt[0].span(), pt[1].span(), (tokPT[j % 3], 0, 4)])

                def pv(j):
                    r = j - 4 * I
                    pt = PT[j % 3]
                    for i in range(max(r, 0), 4):
                        for m in range(2):
                            bk, aap, (a0, a1) = acc(m, i)
                            first = id(bk) not in started
                            started.add(id(bk))
                            mm(aap, pt[m].ap[:, 128 * i:128 * i + 128], V1.ap[:, j, h, :], first, j == 4 * I + i,
                               [(tokPT[j % 3], 0, 4), pt[0].span(), pt[1].span(), V1.span(j, h)],
                               [banks[4].span(), banks[5].span(), banks[6].span(), (tokACC, 0, 4)],
                               skip_group_check=True)

                qk(0)
                for j in range(last + 1):
                    if j + 1 <= last:
                        qk(j + 1)
                    stepc[0] += 1
                    if inj and stepc[0] % 2 == 0:
                        inj.pop(0)()
                    tick_pending()
                    pv(j)

                flush_pending()
                for k in range(3):
                    P.op("dve", lambda e, k=k: e.tensor_copy(out=ACCS[k].ap, in_=banks[4 + k].ap[:, 0:ACCS[k].shape[0]]),
                         reads=[(tokACC, 0, 4), banks[4 + k].span()], writes=[ACCS[k].span()])

                oab = OAB[I % 2]
                for i in range(4):
                    par = i % 2
                    if i < 3:
                        a0, a1 = ACCS[0].ap[:, 129 * i:129 * i + 129], ACCS[1].ap[:, 129 * i:129 * i + 129]
                        rd0, rd1 = ACCS[0].span((129 * i, 129 * i + 129)), ACCS[1].span((129 * i, 129 * i + 129))
                    else:
                        a0, a1 = ACCS[2].ap[:, 0:129], ACCS[2].ap[:, 129:258]
                        rd0, rd1 = ACCS[2].span((0, 129)), ACCS[2].span((129, 258))
                    cs = 24 + 3 * par
                    sm = lambda k, cs=cs: small.ap[:, cs + k:cs + k + 1]
                    sms = lambda k, cs=cs: small.span((cs + k, cs + k + 1))
                    P.op("dve", lambda e, a0=a0, sm=sm: e.reciprocal(out=sm(0), in_=a0[:, 128:129]),
                         reads=[rd0], writes=[sms(0)])
                    P.op("dve", lambda e, a1=a1, sm=sm: e.reciprocal(out=sm(1), in_=a1[:, 128:129]),
                         reads=[rd1], writes=[sms(1)])
                    P.op("dve", lambda e, sm=sm: e.tensor_tensor(out=sm(2), in0=sm(1), in1=nlam.ap, op=ALU.mult),
                         reads=[sms(1), nlam.span()], writes=[sms(2)])
                    t1, o = FT1[par], FO[i]
                    P.op("dve", lambda e, a0=a0, sm=sm, t1=t1: e.tensor_scalar(
                        out=t1.ap, in0=a0[:, 0:128], scalar1=sm(0), scalar2=None, op0=ALU.mult),
                        reads=[rd0, sms(0)], writes=[t1.span()])
                    P.op("dve", lambda e, a1=a1, sm=sm, t1=t1, o=o: e.scalar_tensor_tensor(
                        out=o.ap, in0=a1[:, 0:128], scalar=sm(2), in1=t1.ap, op0=ALU.mult, op1=ALU.add),
                        reads=[rd1, sms(2), t1.span()], writes=[o.span()])
                    P.op("dve", lambda e, t1=t1, o=o: e.tensor_tensor(out=t1.ap, in0=o.ap, in1=o.ap, op=ALU.mult),
                         reads=[o.span()], writes=[t1.span()])
                    P.op("dve", lambda e, t1=t1, i=i: e.reduce_sum(out=small.ap[:, 16 + i:17 + i], in_=t1.ap, axis=AX.X),
                         reads=[t1.span()], writes=[small.span((16 + i, 17 + i))])

                def part_b():
                    rstd_from(small.ap[:, 16:20], small.span((16, 20)), small.ap[:, 20:24], small.span((20, 24)), 128,
                              RMS_EPS)

                def part_c(oab=oab, I=I):
                    for i in range(4):
                        o = FO[i]
                        P.op("dve", lambda e, o=o, i=i: e.scalar_tensor_tensor(
                            out=oab.ap[:, i, :], in0=o.ap, scalar=small.ap[:, 20 + i:21 + i], in1=gsub.ap,
                            op0=ALU.mult, op1=ALU.mult),
                            reads=[o.span(), small.span((20 + i, 21 + i)), gsub.span()], writes=[oab.span(i)])
                    tp = next_tp()
                    for i in range(4):
                        mm(tp.ap[:, i * 128:i * 128 + 128], oab.ap[:, i, :], ident.ap, True, True,
                           [oab.span(i), ident.span()], [tp.span()])
                    P.op("dve", lambda e: e.tensor_copy(out=OAT.ap[:, h, 512 * I:512 * I + 512], in_=tp.ap),
                         reads=[tp.span()], writes=[OAT.span(h, (512 * I, 512 * I + 512))])

                pending.append([4, part_b])
                pending.append([7, part_c])
            while inj:
                inj.pop(0)()

        def linear_fm2(specs, nchunks, consume):
            for g0 in range(0, nchunks, 4):
                n = min(4, nchunks - g0)
                sls = [load_w(wv[:, :, c0 + 128 * g0:c0 + 128 * (g0 + n)], 8, 128 * n) for (wv, c0, _, _) in specs]
                for cc in range(n):
                    pss = []
                    for sl, (wv, c0, rhs, t0) in zip(sls, specs):
                        ps = next_bank()
                        for kc in range(8):
                            mm(ps.ap, sl.ap[:, kc, 128 * cc:128 * cc + 128], rhs.ap[:, kc, t0:t0 + 512], kc == 0,
                               kc == 7, [sl.span(), rhs.span(kc, (t0, t0 + 512))], [ps.span()])
                        pss.append(ps)
                    consume(g0 + cc, *pss)

        def make_gbc(dst, ch0, b):
            DG = [carve(57344 + 512 * i, [128], F32) for i in range(2)]
            for half in range(2):
                ps = next_bank()
                for cc in range(4):
                    c = half * 4 + cc
                    dg = DG[c % 2]
                    P.op("dve", lambda e, dg=dg, c=c: e.tensor_scalar(
                        out=dg.ap, in0=ident_f.ap, scalar1=modT.ap[:, ch0 + c, b:b + 1], scalar2=None,
                        op0=ALU.mult),
                        reads=[ident_f.span(), modT.span()], writes=[dg.span()])
                    mm(ps.ap[:, cc * 128:cc * 128 + 128], ones_f.ap, dg.ap, True, True,
                       [ones_f.span(), dg.span()], [ps.span()])
                P.op("dve", lambda e, ps=ps, half=half: e.tensor_copy(out=dst.ap[:, half * 512:half * 512 + 512],
                                                                      in_=ps.ap),
                     reads=[ps.span()], writes=[dst.span((half * 512, half * 512 + 512))])

        P.op("sp", lambda e: e.dma_start(out=bc[0].ap, in_=bcast(lng_d, 0, D)), writes=[bc[0].span()], dma="new")
        P.op("sp", lambda e: e.dma_start(out=bc[1].ap, in_=bcast(lnb_d, 0, D)), writes=[bc[1].span()], dma="new")
        P.op("sp", lambda e: e.dma_start(out=bc[2].ap, in_=bcast(bsp_d, 0, D)), writes=[bc[2].span()], dma="new")
        P.op("sp", lambda e: e.dma_start(out=bc[5].ap, in_=bcast(gf_d, 0, D)), writes=[bc[5].span()], dma="new")

        def tail(b, pc):
            tok0 = pc * 512
            x1p = carve(0, [4, D], F32)
            S1 = carve(16384, [8, 512], BF16)
            S2 = carve(24576, [8, 512], BF16)
            S3 = carve(32768, [8, 512], BF16)
            vLN = carve(24576, [4, D], BF16)
            GZ = [carve(40960 + 4096 * i, [D], F32) for i in range(2)]
            SG = [carve(49152 + 2048 * i, [512], F32) for i in range(2)]
            TMP = [carve(53248 + 2048 * i, [512], F32) for i in range(2)]
            YB = [carve(57344 + 4096 * i, [D], F32) for i in range(2)]
            h2T = carve(26624, [8, 512], BF16)
            actT = carve(34816, [11, 512], BF16)
            cnt = {"sg": 0, "tmp": 0}

            def nsg():
                cnt["sg"] += 1
                return SG[cnt["sg"] % 2]

            def ntmp():
                cnt["tmp"] += 1
                return TMP[cnt["tmp"] % 2]

            def c_u(c, ps):
                P.op("act", lambda e: e.activation(out=S1.ap[:, c, :], in_=ps.ap, func=AF.Gelu_apprx_tanh),
                     reads=[ps.span()], writes=[S1.span(c)])
            linear_fm2([(win_v, 3072, hT, tok0)], 8, c_u)

            wg = [load_w(win_v[:, :, 4096 + 512 * i:4096 + 512 * i + 512], 8, 512) for i in range(2)]
            m1_slots = [rr["slot"] % NSLOT, (rr["slot"] + 1) % NSLOT]
            S4 = carve(65536, [8, 512], BF16)
            for tb in range(4):
                gz = GZ[tb % 2]
                t0 = tok0 + tb * 128
                for half in range(2):
                    ps = next_bank()
                    for kc in range(8):
                        mm(ps.ap, hT.ap[:, kc, t0:t0 + 128], wg[half].ap[:, kc, :], kc == 0, kc == 7,
                           [hT.span(kc, (t0, t0 + 128)), wg[half].span()], [ps.span()])
                    P.op("act", lambda e, ps=ps, gz=gz, half=half: e.activation(
                        out=gz.ap[:, half * 512:half * 512 + 512], in_=ps.ap, func=AF.Gelu_apprx_tanh),
                        reads=[ps.span()], writes=[gz.span((half * 512, half * 512 + 512))])
                for half in range(2):
                    P.op("dve", lambda e, gz=gz, half=half: e.bn_stats(
                        out=small.ap[:, 32 + 6 * half:38 + 6 * half], in_=gz.ap[:, half * 512:half * 512 + 512]),
                        reads=[gz.span((half * 512, half * 512 + 512))],
                        writes=[small.span((32 + 6 * half, 38 + 6 * half))])
                P.op("dve", lambda e: e.bn_aggr(out=small.ap[:, 44:46], in_=small.ap[:, 32:44]),
                     reads=[small.span((32, 44))], writes=[small.span((44, 46))])
                rstd_from(small.ap[:, 45:46], small.span((45, 46)), small.ap[:, 46:47], small.span((46, 47)), 1, LN_EPS)
                P.op("dve", lambda e: e.scalar_tensor_tensor(out=small.ap[:, 47:48], in0=small.ap[:, 44:45],
                                                             scalar=-1.0, in1=small.ap[:, 46:47], op0=ALU.mult,
                                                             op1=ALU.mult),
                     reads=[small.span((44, 47))], writes=[small.span((47, 48))])
                P.op("act", lambda e, gz=gz: e.activation(out=gz.ap, in_=gz.ap, func=AF.Identity,
                                                          bias=small.ap[:, 47:48], scale=small.ap[:, 46:47]),
                     reads=[gz.span(), small.span((46, 48))], writes=[gz.span()])
                P.op("dve", lambda e, gz=gz: e.tensor_tensor(out=gz.ap, in0=gz.ap, in1=bc[0].ap, op=ALU.mult),
                     reads=[gz.span(), bc[0].span()], writes=[gz.span()])
                P.op("dve", lambda e, gz=gz, tb=tb: e.tensor_tensor(out=vLN.ap[:, tb, :], in0=gz.ap, in1=bc[1].ap,
                                                                    op=ALU.add),
                     reads=[gz.span(), bc[1].span()], writes=[vLN.span(tb)])
                slm = wslot[m1_slots[tb % 2]]
                ca = 256 * tb
                P.op("pool", lambda e, slm=slm, ca=ca: e.dma_start(out=slm.ap[:, :, 0:256], in_=wpa_v[:, :, ca:ca + 256]),
                     writes=[slm.span()], dma="w%d" % m1_slots[tb % 2])
                P.op("pool", lambda e, slm=slm, ca=ca: e.dma_start(out=slm.ap[:, :, 256:512],
                                                                   in_=win_v[:, :, 5120 + ca:5120 + ca + 256]),
                     writes=[slm.span()], dma="w%d" % m1_slots[tb % 2])
                for cc in range(2):
                    c = 2 * tb + cc
                    ps_a, ps_g = next_bank(), next_bank()
                    for kc in range(8):
                        mm(ps_a.ap, slm.ap[:, kc, 128 * cc:128 * cc + 128], OAT.ap[:, kc, tok0:tok0 + 512], kc == 0,
                           kc == 7, [slm.span(), OAT.span(kc, (tok0, tok0 + 512))], [ps_a.span()])
                    for kc in range(8):
                        mm(ps_g.ap, slm.ap[:, kc, 256 + 128 * cc:256 + 128 * cc + 128], hT.ap[:, kc, tok0:tok0 + 512],
                           kc == 0, kc == 7, [slm.span(), hT.span(kc, (tok0, tok0 + 512))], [ps_g.span()])
                    sg = nsg()
                    P.op("act", lambda e, sg=sg, ps_g=ps_g: e.activation(out=sg.ap, in_=ps_g.ap, func=AF.Sigmoid),
                         reads=[ps_g.span()], writes=[sg.span()])
                    P.op("dve", lambda e, sg=sg, ps_a=ps_a, c=c: e.tensor_tensor(out=S4.ap[:, c, :], in0=sg.ap,
                                                                                 in1=ps_a.ap, op=ALU.mult),
                         reads=[sg.span(), ps_a.span()], writes=[S4.span(c)])
            rr["slot"] += 2

            for g in range(8):
                ps = next_bank()
                for tb in range(4):
                    mm(ps.ap[:, tb * 128:tb * 128 + 128], vLN.ap[:, tb, g * 128:g * 128 + 128], wsT.ap[:, g, :],
                       True, True, [vLN.span(tb), wsT.span(g)], [ps.span()])
                tmp = ntmp()
                for tb in range(4):
                    P.op("dve", lambda e, ps=ps, tmp=tmp, tb=tb, g=g: e.tensor_tensor(
                        out=tmp.ap[:, tb * 128:tb * 128 + 128], in0=ps.ap[:, tb * 128:tb * 128 + 128],
                        in1=bc[2].ap[:, g * 128:g * 128 + 128], op=ALU.add),
                        reads=[ps.span(), bc[2].span()],
                        writes=[tmp.span((tb * 128, tb * 128 + 128))])
                P.op("dve", lambda e, tmp=tmp, g=g: e.tensor_tensor(out=S3.ap[:, g, :], in0=tmp.ap, in1=S1.ap[:, g, :],
                                                                    op=ALU.mult),
                     reads=[tmp.span(), S1.span(g)], writes=[S3.span(g)])

            def c_b(c, ps_b, ps_g):
                sg, tmp = nsg(), ntmp()
                P.op("act", lambda e: e.activation(out=sg.ap, in_=ps_g.ap, func=AF.Sigmoid),
                     reads=[ps_g.span()], writes=[sg.span()])
                P.op("dve", lambda e: e.tensor_tensor(out=tmp.ap, in0=sg.ap, in1=ps_b.ap, op=ALU.mult),
                     reads=[sg.span(), ps_b.span()], writes=[tmp.span()])
                P.op("dve", lambda e: e.tensor_tensor(out=S2.ap[:, c, :], in0=tmp.ap, in1=S4.ap[:, c, :], op=ALU.add),
                     reads=[tmp.span(), S4.span(c)], writes=[S2.span(c)])
            linear_fm2([(wpb_v, 0, S3, 0), (win_v, 6144, hT, tok0)], 8, c_b)

            wo = [load_w(wout_v[:, :, 512 * i:512 * i + 512], 8, 512) for i in range(2)]
            for tb in range(4):
                xb = GZ[tb % 2]
                t0 = tok0 + tb * 128
                P.op("sp", lambda e, xb=xb, t0=t0: e.dma_start(out=xb.ap, in_=x_d[b, t0:t0 + 128, :]),
                     writes=[xb.span()], dma="gz%d" % (tb % 2))
                for dh in range(2):
                    ps = next_bank()
                    for kc in range(8):
                        mm(ps.ap, S2.ap[:, kc, tb * 128:tb * 128 + 128], wo[dh].ap[:, kc, :], kc == 0, kc == 7,
                           [S2.span(kc, (tb * 128, tb * 128 + 128)), wo[dh].span()], [ps.span()])
                    tmp = ntmp()
                    P.op("dve", lambda e, ps=ps, tmp=tmp, dh=dh: e.tensor_tensor(
                        out=tmp.ap, in0=ps.ap, in1=bc[3].ap[:, dh * 512:dh * 512 + 512], op=ALU.mult),
                        reads=[ps.span(), bc[3].span()], writes=[tmp.span()])
                    P.op("dve", lambda e, tmp=tmp, xb=xb, tb=tb, dh=dh: e.tensor_tensor(
                        out=x1p.ap[:, tb, dh * 512:dh * 512 + 512], in0=tmp.ap, in1=xb.ap[:, dh * 512:dh * 512 + 512],
                        op=ALU.add),
                        reads=[tmp.span(), xb.span((dh * 512, dh * 512 + 512))],
                        writes=[x1p.span(tb, (dh * 512, dh * 512 + 512))])

            if dbg_stage == "x1" and b == 0 and pc == 0:
                dump("x1", x1p.ap, [128, 4, D], F32, x1p.span())
                return

            if dbg_stage == "e_g":
                dump("bc0", bc[4].ap, [128, D], F32, bc[4].span())
                dump("bc1", bc[5].ap, [128, D], F32, bc[5].span())
                raise StopIteration
            norm_to_T(lambda tb: x1p.ap[:, tb, :], lambda tb: x1p.span(tb), h2T, 0, gm2, 24, b, 16384)
            if dbg_stage == "e_h2T":
                dump("h2T", h2T.ap, [128, 8, 512], BF16, h2T.span())
                dump("bc0", bc[4].ap, [128, D], F32, bc[4].span())
                raise StopIteration
            for hh in range(2):
                for g0 in (0, 4, 8):
                    n = min(4, 11 - g0)
                    cg = hh * 1408 + 128 * g0
                    slg = load_w(wfi_v[:, :, cg:cg + 128 * n], 8, 128 * n)
                    slu = load_w(wfi_v[:, :, FF + cg:FF + cg + 128 * n], 8, 128 * n)
                    for cc in range(n):
                        psg, psu = next_bank(), next_bank()
                        for ps, sl in ((psg, slg), (psu, slu)):
                            for kc in range(8):
                                mm(ps.ap, sl.ap[:, kc, 128 * cc:128 * cc + 128], h2T.ap[:, kc, :], kc == 0, kc == 7,
                                   [sl.span(), h2T.span(kc)], [ps.span()])
                        sg = nsg()
                        P.op("act", lambda e, sg=sg, psg=psg: e.activation(out=sg.ap, in_=psg.ap, func=AF.Silu),
                             reads=[psg.span()], writes=[sg.span()])
                        c = g0 + cc
                        P.op("dve", lambda e, sg=sg, psu=psu, c=c: e.tensor_tensor(out=actT.ap[:, c, :], in0=sg.ap,
                                                                                   in1=psu.ap, op=ALU.mult),
                             reads=[sg.span(), psu.span()], writes=[actT.span(c)])
                for dh in range(2):
                    wa = load_w(wfo_v[:, hh * 11:hh * 11 + 8, dh * 512:dh * 512 + 512], 8, 512)
                    wb = load_w(wfo_v[:, hh * 11 + 8:hh * 11 + 11, dh * 512:dh * 512 + 512], 3, 512)
                    for tb in range(4):
                        ps = next_bank()
                        for kc in range(11):
                            w_ap = wa.ap[:, kc, :] if kc < 8 else wb.ap[:, kc - 8, :]
                            w_sp = wa.span() if kc < 8 else wb.span()
                            mm(ps.ap, actT.ap[:, kc, tb * 128:tb * 128 + 128], w_ap, kc == 0, kc == 10,
                               [actT.span(kc, (tb * 128, tb * 128 + 128)), w_sp], [ps.span()])
                        tmp = ntmp()
                        P.op("dve", lambda e, ps=ps, tmp=tmp, dh=dh: e.tensor_tensor(
                            out=tmp.ap, in0=ps.ap, in1=bc[4].ap[:, dh * 512:dh * 512 + 512], op=ALU.mult),
                            reads=[ps.span(), bc[4].span()], writes=[tmp.span()])
                        xs_ = x1p.span(tb, (dh * 512, dh * 512 + 512))
                        P.op("dve", lambda e, tmp=tmp, tb=tb, dh=dh: e.tensor_tensor(
                            out=x1p.ap[:, tb, dh * 512:dh * 512 + 512], in0=x1p.ap[:, tb, dh * 512:dh * 512 + 512],
                            in1=tmp.ap, op=ALU.add),
                            reads=[tmp.span(), xs_], writes=[xs_])
            if dbg_stage == "e_ffn":
                dump("x2", x1p.ap, [128, 4, D], F32, x1p.span())
                raise StopIteration
            fj = carve(16384, [D], BF16)
            for tb in range(4):
                P.op("act", lambda e, tb=tb: e.activation(out=fj.ap, in_=x1p.ap[:, tb, :], func=AF.Square,
                                                          accum_out=small.ap[:, 48 + tb:49 + tb]),
                     reads=[x1p.span(tb)], writes=[fj.span(), small.span((48 + tb, 49 + tb))])
            rstd_from(small.ap[:, 48:52], small.span((48, 52)), small.ap[:, 52:56], small.span((52, 56)), D, RMS_EPS)
            for tb in range(4):
                par = tb % 2
                yb = YB[par]
                t0 = tok0 + tb * 128
                P.op("dve", lambda e, tb=tb, yb=yb: e.scalar_tensor_tensor(
                    out=yb.ap, in0=x1p.ap[:, tb, :], scalar=small.ap[:, 52 + tb:53 + tb], in1=bc[5].ap, op0=ALU.mult,
                    op1=ALU.mult),
                    reads=[x1p.span(tb), small.span((52 + tb, 53 + tb)), bc[5].span()], writes=[yb.span()])
                P.op("sp", lambda e, yb=yb, t0=t0: e.dma_start(out=y_d[b, t0:t0 + 128, :], in_=yb.ap),
                     reads=[yb.span()], dma="yb%d" % par, final=True)

        def run_all():
            for b in range(NB):
                XB = [carve(36864 + 4096 * i, [D], F32) for i in range(4)]
                for tg in range(4):
                    for tb in range(4):
                        t = tg * 4 + tb
                        xb = XB[t % 4]
                        P.op("sp", lambda e, xb=xb, t=t, b=b: e.dma_start(out=xb.ap, in_=x_d[b, t * 128:(t + 1) * 128, :]),
                             writes=[xb.span()], dma="xb%d" % (t % 4))
                    norm_to_T(lambda tb: XB[tb].ap, lambda tb: XB[tb].span(),
                              hT, tg * 512, gm1, 0, b, 53248)
                    if dbg_stage == "a2":
                        for kc in range(8):
                            dump("hT%d" % kc, hT.ap[:, kc, 0:512], [128, 512], BF16, hT.span(kc, (0, 512)))
                        raise StopIteration
                if dbg_stage == "a3":
                    for kc in range(8):
                        dump("hT%d" % kc, hT.ap[:, kc, 1536:2048], [128, 512], BF16, hT.span(kc, (1536, 2048)))
                    raise StopIteration
                if dbg_stage == "hT":
                    dump("hT", hT.ap, [128, 8, S], BF16, hT.span())
                    dump("modT", modT.ap, [128, 48, NB], F32, modT.span())
                    dump("BT", BT.ap, [128, 8, 256], BF16, BT.span())
                    dump("wsT", wsT.ap, [128, 8, 128], BF16, wsT.span())
                    dump("nlam", nlam.ap, [128, 1], F32, nlam.span())
                    return

                P.op("dve", lambda e: e.memset(V1.ap[:, :, :, 128:129], 1.0), writes=[V1.span()])
                wv = [load_w(win_v[:, :, 2048 + 512 * i:2048 + 512 * i + 512], 8, 512) for i in range(2)]
                for tb in range(16):
                    for half in range(2):
                        ps = next_bank()
                        for kc in range(8):
                            mm(ps.ap, hT.ap[:, kc, tb * 128:tb * 128 + 128], wv[half].ap[:, kc, :], kc == 0, kc == 7,
                               [hT.span(kc, (tb * 128, tb * 128 + 128)), wv[half].span()], [ps.span()])
                        dst = V1.ap[:, tb, 4 * half:4 * half + 4, 0:128]
                        src = ps.ap.rearrange("p (h e) -> p h e", e=128)
                        if (tb + half) % 2 == 0:
                            P.op("act", lambda e, dst=dst, src=src: e.copy(out=dst, in_=src), reads=[ps.span()],
                                 writes=[V1.span(tb, (4 * half, 4 * half + 4))])
                        else:
                            P.op("dve", lambda e, dst=dst, src=src: e.tensor_copy(out=dst, in_=src), reads=[ps.span()],
                                 writes=[V1.span(tb, (4 * half, 4 * half + 4))])

                def load_wqk(h):
                    s = h % 2
                    for which in range(2):
                        c0 = which * 1024 + h * 128
                        P.op("pool", lambda e, s=s, which=which, c0=c0: e.dma_start(
                            out=WQK[s].ap[:, which, :, :], in_=win_v[:, :, c0:c0 + 128]),
                            writes=[WQK[s].span(which)], dma="wqk%d%d" % (s, which))

                def qk_groups(h):
                    s = h % 2
                    gs = []
                    for which in range(2):
                        for t8 in range(8):
                            def g(which=which, t8=t8, s=s):
                                ps = banks[7]
                                a, bnd = t8 * 256, t8 * 256 + 256
                                for kc in range(8):
                                    mm(ps.ap[:, 0:256], WQK[s].ap[:, which, kc, :], hT.ap[:, kc, a:bnd],
                                       kc == 0, kc == 7,
                                       [WQK[s].span(which), hT.span(kc, (a, bnd))], [ps.span()])
                                dst = QK[s].ap[:, which, a:bnd]
                                P.op("dve", lambda e, dst=dst, ps=ps: e.tensor_copy(out=dst, in_=ps.ap[:, 0:256]),
                                     reads=[ps.span()], writes=[QK[s].span(which, (a, bnd))])
                            gs.append(g)
                    return gs

                load_wqk(0)
                for g in qk_groups(0):
                    g()
                for h in range(H):
                    s = h % 2
                    inj = []
                    if h + 1 < H:
                        load_wqk(h + 1)
                        inj = qk_groups(h + 1)
                    if dbg_stage == "qk":
                        for q4 in range(4):
                            dump("QT%d" % q4, QK[0].ap[:, 0, q4 * 512:q4 * 512 + 512], [128, 512], BF16,
                                 QK[0].span(0, (q4 * 512, q4 * 512 + 512)))
                        for tb in range(2):
                            dump("V%d" % tb, V1.ap[:, tb, :, :], [128, 8, 129], BF16, V1.span(tb))
                        return
                    attention(b, h, s, inj)
                    if dbg_stage == "att1":
                        flush_pending()
                        for q4 in range(4):
                            dump("OAT%d" % q4, OAT.ap[:, 0, q4 * 512:q4 * 512 + 512], [128, 512], BF16,
                                 OAT.span(0, (q4 * 512, q4 * 512 + 512)))
                            dump("QT%d" % q4, QK[0].ap[:, 0, q4 * 512:q4 * 512 + 512], [128, 512], BF16,
                                 QK[0].span(0, (q4 * 512, q4 * 512 + 512)))
                        return
                flush_pending()
                if dbg_stage == "att":
                    dump("OAT", OAT.ap, [128, 8, S], BF16, OAT.span())
                    return

                make_gbc(bc[3], 16, b)
                make_gbc(bc[4], 40, b)
                for pc in range(4):
                    tail(b, pc)
                    if dbg_stage == "x1":
                        return
                    if dbg_stage == "p0":
                        return
                    if dbg_stage == "p1" and pc == 1:
                        return
                if dbg_stage == "b0":
                    return

        try:
            run_all()
        except StopIteration:
            pass
        P.emit()
    return nc, P, dbg


def make_in_maps(inp):
    f = lambda a: np.ascontiguousarray(np.asarray(a, dtype=np.float32))
    x, c = f(inp["x"]), f(inp["c"])
    shared = {
        "w_ada": f(inp["w_ada"][0]),
        "b_ada": f(inp["b_ada"][0]).reshape(1, 6 * D),
        "b_adaT": f(f(inp["b_ada"][0]).reshape(48, 128).T),
        "n1gT": f(f(inp["norm1_g"][0]).reshape(8, 128).T),
        "n2gT": f(f(inp["norm2_g"][0]).reshape(8, 128).T),
        "w_in": f(inp["w_in"][0]),
        "lam4": f(np.stack([f(inp["lambda_q1"][0]), f(inp["lambda_k1"][0]), f(inp["lambda_q2"][0]),
                            f(inp["lambda_k2"][0])])).reshape(1, 256),
        "subln_g": f(inp["subln_g"][0]).reshape(1, 128),
        "ln_v_g": f(inp["ln_v_g"][0]).reshape(1, D),
        "ln_v_b": f(inp["ln_v_b"][0]).reshape(1, D),
        "w_spT": f(f(inp["w_spatial"][0]).transpose(0, 2, 1)),
        "b_sp": f(inp["b_spatial"][0]).reshape(1, D),
        "w_proj_a": f(inp["w_proj_a"][0]),
        "w_proj_b": f(inp["w_proj_b"][0]),
        "w_out": f(inp["w_out"][0]),
        "w_ffn_in": f(inp["w_ffn_in"][0]),
        "w_ffn_out": f(inp["w_ffn_out"][0]),
        "rel_biasT": f(f(inp["rel_bias"]).T),
        "rel_bias": f(inp["rel_bias"]),
        "final_g": f(inp["final_g"]).reshape(1, D),
    }
    maps = []
    for i in range(NCORES):
        m = dict(shared)
        m["x"] = f(x[NB * i:NB * i + NB])
        m["cT"] = f(c[NB * i:NB * i + NB].reshape(NB, 8, 128).transpose(2, 1, 0))
        maps.append(m)
    return maps


_CACHE = {}


def kernel(**inputs):
    if "nc" not in _CACHE:
        _CACHE["nc"] = build()[0]
    nc = _CACHE["nc"]
    maps = make_in_maps(inputs)
    res = run_bass_kernel_spmd(nc, maps, core_ids=list(range(NCORES)))
    return np.concatenate([np.asarray(r["y"], dtype=np.float32) for r in res.results], axis=0)
```

```python
import math
from contextlib import ExitStack

import numpy as np
import concourse.bass as bass
import concourse.mybir as mybir
from concourse.bass_utils import run_bass_kernel_spmd

F32 = mybir.dt.float32
BF16 = mybir.dt.bfloat16
AF = mybir.ActivationFunctionType
ALU = mybir.AluOpType
AX = mybir.AxisListType

NCORES = 8
NB = 2
S = 2048
D = 1024
H = 8
FF = 2816
RMS_EPS = 1e-6
LN_EPS = 1e-5
LAM_INIT = 0.2
NEG = -30000.0
ENGS = ["pe", "act", "dve", "pool", "sp"]


class Op:
    __slots__ = ("eng", "fn", "deps", "marked", "seq", "dma", "dma_val", "pos")

    def __init__(self, eng, fn, dma):
        self.eng, self.fn, self.dma = eng, fn, dma
        self.deps, self.marked, self.seq, self.dma_val = [], False, 0, 0


class Trk:
    def __init__(self, name, nbytes):
        self.name = name
        self.segs = [[0, nbytes, None, {}]]

    def access(self, lo, hi, op, write):
        deps, new = [], []
        for seg in self.segs:
            s_lo, s_hi, w, rs = seg
            if s_hi <= lo or s_lo >= hi:
                new.append(seg)
                continue
            if s_lo < lo:
                new.append([s_lo, lo, w, dict(rs)])
            m_lo, m_hi = max(s_lo, lo), min(s_hi, hi)
            if write:
                if w is not None:
                    deps.append(("waw", w))
                deps += [("war", r) for r in rs.values()]
                new.append([m_lo, m_hi, op, {}])
            else:
                if w is not None:
                    deps.append(("raw", w))
                rs2 = dict(rs)
                rs2[op.dma if op.dma is not None else op.eng] = op
                new.append([m_lo, m_hi, w, rs2])
            if s_hi > hi:
                new.append([hi, s_hi, w, dict(rs)])
        merged = []
        for seg in new:
            if merged and merged[-1][1] == seg[0] and merged[-1][2] is seg[2] and merged[-1][3] == seg[3]:
                merged[-1][1] = seg[1]
            else:
                merged.append(seg)
        self.segs = merged
        return deps


class V:
    def __init__(self, ap, trk, base, esz, shape):
        self.ap, self.trk, self.base, self.esz, self.shape = ap, trk, base, esz, list(shape)
        st, acc = [], 1
        for n in reversed(self.shape):
            st.append(acc)
            acc *= n
        self.strides = list(reversed(st))
        self.total = acc

    def span(self, *idx):
        lo = hi = 0
        for d, n in enumerate(self.shape):
            ix = idx[d] if d < len(idx) else None
            if ix is None:
                a, b = 0, n
            elif isinstance(ix, tuple):
                a, b = ix
            else:
                a, b = ix, ix + 1
            lo += a * self.strides[d]
            hi += (b - 1) * self.strides[d]
        return (self.trk, self.base + lo * self.esz, self.base + (hi + 1) * self.esz)


class Prog:
    def __init__(self, nc):
        self.nc = nc
        self.ops = {e: [] for e in ENGS}
        self.dma_cnt = {}
        self.n_anon = 0
        self.final = []

    def op(self, eng, fn, reads=(), writes=(), dma=None, final=False):
        if dma == "new":
            self.n_anon += 1
            dma = "anon%d" % self.n_anon
        o = Op(eng, fn, dma)
        deps = []
        for (t, lo, hi) in reads:
            deps += t.access(lo, hi, o, False)
        for (t, lo, hi) in writes:
            deps += t.access(lo, hi, o, True)
        best = {}
        for kind, p in deps:
            if p is o:
                continue
            if p.dma is None and o.dma is None and p.eng == eng:
                if eng == "pe" or kind != "raw":
                    continue
            key = ("d", p.dma) if p.dma is not None else ("e", p.eng)
            rank = p.dma_val if p.dma is not None else p.pos
            if key not in best or rank > best[key][0]:
                best[key] = (rank, p)
        for _, p in best.values():
            o.deps.append(p)
            if p.dma is None:
                p.marked = True
        o.pos = len(self.ops[eng])
        if dma is not None:
            self.dma_cnt[dma] = self.dma_cnt.get(dma, 0) + 1
            o.dma_val = 16 * self.dma_cnt[dma]
        self.ops[eng].append(o)
        if final:
            self.final.append(o)
        return o

    def emit(self):
        nc = self.nc
        for e in ENGS:
            n = 0
            for o in self.ops[e]:
                if o.dma is None and o.marked:
                    n += 1
                    o.seq = n
        needed = set()
        for e in ENGS:
            seen = {}
            deps_iter = [p for o in self.ops[e] for p in o.deps]
            if e == "sp":
                deps_iter += list(self.final)
            for p in deps_iter:
                key, val = (("d", p.dma), p.dma_val) if p.dma is not None else (("e", p.eng), p.seq)
                if seen.get(key, 0) >= val:
                    continue
                seen[key] = val
                if p.dma is None:
                    needed.add(id(p))
        for e in ENGS:
            n = 0
            for o in self.ops[e]:
                if o.dma is None:
                    o.marked = id(o) in needed
                    o.seq = 0
                    if o.marked:
                        n += 1
                        o.seq = n
            self.n_marks = getattr(self, "n_marks", {})
            self.n_marks[e] = n
        with ExitStack() as es:
            LIMIT = 900
            esem = {e: [es.enter_context(nc.semaphore("sem_%s_%d" % (e, k)))
                        for k in range(self.n_marks[e] // LIMIT + 1)] for e in ENGS}
            dsem = {k: es.enter_context(nc.semaphore("dma_" + k)) for k in self.dma_cnt}
            block = es.enter_context(nc.Block())
            stats = {}

            def run(ename, eng):
                seen = {}
                nwait = 0
                ops = self.ops[ename]

                def wait_for(p):
                    nonlocal nwait
                    if p.dma is not None:
                        key, gval = ("d", p.dma), p.dma_val
                    else:
                        key, gval = ("e", p.eng), p.seq
                    if seen.get(key, 0) >= gval:
                        return
                    seen[key] = gval
                    if p.dma is not None:
                        eng.wait_ge(dsem[p.dma], gval)
                    else:
                        eng.wait_ge(esem[p.eng][(gval - 1) // LIMIT], (gval - 1) % LIMIT + 1)
                    nwait += 1

                for o in ops:
                    for p in o.deps:
                        wait_for(p)
                    inst = o.fn(eng)
                    if o.dma is not None:
                        inst.then_inc(dsem[o.dma], 16)
                    elif o.marked:
                        inst.then_inc(esem[ename][(o.seq - 1) // LIMIT], 1)
                if ename == "sp":
                    for o in self.final:
                        wait_for(o)
                stats[ename] = (len(ops), nwait)

            @block.tensor
            def _(e):
                run("pe", e)

            @block.scalar
            def _(e):
                run("act", e)

            @block.vector
            def _(e):
                run("dve", e)

            @block.gpsimd
            def _(e):
                run("pool", e)

            @block.sync
            def _(e):
                run("sp", e)

            self.stats = stats


def rel_bucket_ranges():
    d = np.arange(0, 256)
    nf = np.maximum(d, 1).astype(np.float32)
    large = 16 + (np.log(nf / np.float32(16)) / np.float32(math.log(8.0)) * np.float32(16)).astype(np.int32)
    large = np.minimum(large, 31)
    bk = np.where(d < 16, d, large)
    out = []
    for v in range(16, 32):
        idx = np.nonzero(bk == v)[0]
        if len(idx):
            assert idx[-1] - idx[0] + 1 == len(idx)
            out.append((v, int(idx[0]), int(idx[-1]) + 1))
    assert all(bk[113:] == 31)
    return out


def build(dbg_stage=None):
    nc = bass.Bass("TRN2", target_bir_lowering=False)
    P = Prog(nc)

    def din(name, shape, dt=F32):
        return nc.dram_tensor(name, list(shape), dt, kind="ExternalInput").ap()

    x_d = din("x", [NB, S, D])
    cT_d = din("cT", [128, 8, NB])
    wada_d = din("w_ada", [D, 6 * D])
    bada_d = din("b_ada", [1, 6 * D])
    badaT_d = din("b_adaT", [128, 48])
    n1gT_d = din("n1gT", [128, 8])
    n2gT_d = din("n2gT", [128, 8])
    win_d = din("w_in", [D, 7 * D])
    lam4_d = din("lam4", [1, 256])
    subg_d = din("subln_g", [1, 128])
    lng_d = din("ln_v_g", [1, D])
    lnb_d = din("ln_v_b", [1, D])
    wspT_d = din("w_spT", [8, 128, 128])
    bsp_d = din("b_sp", [1, D])
    wpa_d = din("w_proj_a", [D, D])
    wpb_d = din("w_proj_b", [D, D])
    wout_d = din("w_out", [D, D])
    wfi_d = din("w_ffn_in", [D, 2 * FF])
    wfo_d = din("w_ffn_out", [FF, D])
    rbT_d = din("rel_biasT", [8, 32])
    rb_d = din("rel_bias", [32, 8])
    gf_d = din("final_g", [1, D])
    y_d = nc.dram_tensor("y", [NB, S, D], F32, kind="ExternalOutput").ap()
    Fd = nc.dram_tensor("Fd_scratch", [8, 384], BF16, kind="Internal").ap()
    gsc = nc.dram_tensor("g_scratch", [2 * NB, D], F32, kind="Internal").ap()

    def bcast(ap2d, off, n):
        return bass.AP(ap2d.tensor, off, [[0, 128], [1, n]])

    win_v = win_d.rearrange("(k p) n -> p k n", p=128)
    wada_v = wada_d.rearrange("(k p) n -> p k n", p=128)
    wpa_v = wpa_d.rearrange("(k p) n -> p k n", p=128)
    wpb_v = wpb_d.rearrange("(k p) n -> p k n", p=128)
    wout_v = wout_d.rearrange("(k p) n -> p k n", p=128)
    wfi_v = wfi_d.rearrange("(k p) n -> p k n", p=128)
    wfo_v = wfo_d.rearrange("(k p) n -> p k n", p=128)

    def scr(name, shape):
        return nc.dram_tensor(name, list(shape), BF16, kind="Internal").ap()

    win_bv = scr("win_b", [D, 7 * D]).rearrange("(k p) n -> p k n", p=128)
    wpa_bv = scr("wpa_b", [D, D]).rearrange("(k p) n -> p k n", p=128)
    wpb_bv = scr("wpb_b", [D, D]).rearrange("(k p) n -> p k n", p=128)
    wout_bv = scr("wout_b", [D, D]).rearrange("(k p) n -> p k n", p=128)
    wfi_bv = scr("wfi_b", [D, 2 * FF]).rearrange("(k p) n -> p k n", p=128)
    wfo_bv = scr("wfo_b", [FF, D]).rearrange("(k p) n -> p k n", p=128)
    cv_jobs = []
    for c0 in range(3072, 7168, 512):
        cv_jobs.append((win_bv[:, :, c0:c0 + 512], win_v[:, :, c0:c0 + 512]))
    for (dv, sv) in ((wpa_bv, wpa_v), (wpb_bv, wpb_v), (wout_bv, wout_v)):
        for c0 in (0, 512):
            cv_jobs.append((dv[:, :, c0:c0 + 512], sv[:, :, c0:c0 + 512]))
    for c0 in range(0, 2 * FF, 512):
        cv_jobs.append((wfi_bv[:, :, c0:c0 + 512], wfi_v[:, :, c0:c0 + 512]))
    for k0, k1 in ((0, 8), (8, 16), (16, 22)):
        for c0 in (0, 512):
            cv_jobs.append((wfo_bv[:, k0:k1, c0:c0 + 512], wfo_v[:, k0:k1, c0:c0 + 512]))
    cv_trk = Trk("cv", 64)
    cv_state = {"i": 0}
    assert len(cv_jobs) <= 64

    def issue_cv(n):
        for _ in range(n):
            i = cv_state["i"]
            if i >= len(cv_jobs):
                return
            cv_state["i"] += 1
            dst, src = cv_jobs[i]
            P.op("pool", lambda e, dst=dst, src=src: e.dma_start(out=dst, in_=src),
                 writes=[(cv_trk, i, i + 1)], dma="cv")

    es = ExitStack()
    with es:
        def sb(name, shape, dt):
            t = es.enter_context(nc.sbuf_tensor("sb_" + name, list(shape), dt))
            esz = 4 if dt == F32 else 2
            n = int(np.prod(shape[1:]))
            ap = t[tuple(slice(None) for _ in shape)]
            return V(ap, Trk(name, n * esz), 0, esz, shape[1:])

        ident_f = sb("ident_f", [128, 128], F32)
        ones_f = sb("ones_f", [128, 128], F32)
        ident = sb("ident", [128, 128], BF16)
        Jb = sb("Jb", [128, 128], BF16)
        wsT = sb("wsT", [128, 8, 128], BF16)
        BT = sb("BT", [128, 8, 256], BF16)
        b31bc = sb("b31bc", [128, 8], F32)
        nlam = sb("nlam", [128, 1], F32)
        gsub = sb("gsub", [128, 128], F32)
        modT = sb("modT", [128, 48, NB], F32)
        gm1 = sb("gm1", [128, 8, NB], F32)
        gm2 = sb("gm2", [128, 8, NB], F32)
        n1gT = sb("n1gT", [128, 8], F32)
        n2gT = sb("n2gT", [128, 8], F32)
        small = sb("small", [128, 64], F32)
        bc = [sb("bc%d" % i, [128, D], F32) for i in range(6)]
        hT = sb("hT", [128, 8, S], BF16)
        OAT = sb("OAT", [128, 8, S], BF16)
        NSLOT = 4
        wslot = [sb("wslot%d" % i, [128, 8, 512], BF16) for i in range(NSLOT)]

        ARENA_E = 37120
        arena_t = es.enter_context(nc.sbuf_tensor("arena", [128, ARENA_E], BF16))
        arena_trk = Trk("arena", ARENA_E * 2)

        def carve(off_b, shape, dt):
            esz = 4 if dt == F32 else 2
            n = int(np.prod(shape))
            assert off_b % 4 == 0 and off_b + n * esz <= ARENA_E * 2, (off_b, shape)
            ap = arena_t[:, off_b // 2: off_b // 2 + n * esz // 2]
            if dt == F32:
                ap = ap.bitcast(F32)
            if len(shape) == 2:
                ap = ap.rearrange("p (a b) -> p a b", b=shape[1])
            elif len(shape) == 3:
                ap = ap.rearrange("p (a b c) -> p a b c", b=shape[1], c=shape[2])
            elif len(shape) == 4:
                ap = ap.rearrange("p (a b c d) -> p a b c d", b=shape[1], c=shape[2], d=shape[3])
            return V(ap, arena_trk, off_b, esz, shape)

        banks = []
        for i in range(8):
            t = es.enter_context(nc.psum_tensor("bank%d" % i, [128, 512], F32))
            banks.append(V(t[:, :], Trk("bank%d" % i, 2048), 0, 4, [512]))

        rr = {"bank": 0, "slot": 0}

        def next_bank(nb=7):
            i = rr["bank"] % nb
            rr["bank"] += 1
            return banks[i]

        w_reads = []

        def load_w(src, kc, n):
            i = rr["slot"] % NSLOT
            rr["slot"] += 1
            sl = wslot[i]
            P.op("pool", lambda e: e.dma_start(out=sl.ap[:, 0:kc, 0:n], in_=src),
                 reads=list(w_reads), writes=[sl.span()], dma="w%d" % i)
            return sl

        def mm(ps_ap, lhsT, rhs, start, stop, reads, writes, **kw):
            P.op("pe", lambda e: e.matmul(ps_ap, lhsT, rhs, start=start, stop=stop, **kw),
                 reads=reads, writes=writes)

        def rstd_from(ss_ap, ss_span, out_ap, out_span, n, eps):
            P.op("act", lambda e: e.activation(out=out_ap, in_=ss_ap, func=AF.Ln, bias=epsc[eps], scale=1.0 / n),
                 reads=[ss_span, small.span((60, 62))], writes=[out_span])
            P.op("act", lambda e: e.activation(out=out_ap, in_=out_ap, func=AF.Exp, scale=-0.5),
                 reads=[out_span], writes=[out_span])

        dbg = []

        def dump(name, view, shape, dt, span):
            d = nc.dram_tensor("dbg_" + name, list(shape), dt, kind="ExternalOutput").ap()
            P.op("sp", lambda e: e.dma_start(out=d, in_=view), reads=[span], dma="new", final=True)
            dbg.append("dbg_" + name)

        P.op("pool", lambda e: e.memset(ones_f.ap, 1.0), writes=[ones_f.span()])
        P.op("pool", lambda e: e.memset(small.ap[:, 60:61], RMS_EPS), writes=[small.span((60, 61))])
        P.op("pool", lambda e: e.memset(small.ap[:, 61:62], LN_EPS), writes=[small.span((61, 62))])
        epsc = {RMS_EPS: small.ap[:, 60:61], LN_EPS: small.ap[:, 61:62]}
        P.op("pool", lambda e: e.affine_select(out=ident_f.ap, in_=ones_f.ap, pattern=[[-1, 128]],
                                               compare_op=ALU.is_equal, fill=0.0, base=0, channel_multiplier=1),
             reads=[ones_f.span()], writes=[ident_f.span()])
        P.op("dve", lambda e: e.tensor_copy(out=ident.ap, in_=ident_f.ap), reads=[ident_f.span()],
             writes=[ident.span()])
        jf = carve(0, [128], F32)
        P.op("pool", lambda e: e.affine_select(out=jf.ap, in_=ones_f.ap, pattern=[[1, 128]],
                                               compare_op=ALU.is_equal, fill=0.0, base=-127, channel_multiplier=1),
             reads=[ones_f.span()], writes=[jf.span()])
        P.op("dve", lambda e: e.tensor_copy(out=Jb.ap, in_=jf.ap), reads=[jf.span()], writes=[Jb.span()])

        wst = carve(512, [8, 128], F32)
        P.op("sp", lambda e: e.dma_start(out=wst.ap, in_=wspT_d.rearrange("g s t -> s g t")),
             writes=[wst.span()], dma="new")
        P.op("pool", lambda e: e.affine_select(out=wsT.ap, in_=wst.ap, pattern=[[0, 8], [1, 128]],
                                               compare_op=ALU.is_ge, fill=0.0, base=0, channel_multiplier=-1),
             reads=[wst.span()], writes=[wsT.span()])

        if dbg_stage == "c1":
            dump("wsT", wsT.ap, [128, 8, 128], BF16, wsT.span())
            dump("ident", ident.ap, [128, 128], BF16, ident.span())
            dump("Jb", Jb.ap, [128, 128], BF16, Jb.span())
            P.emit()
            return nc, P, dbg
        RB = carve(4608, [32], F32)
        Ft = carve(4736, [384], F32)
        Fb = carve(6272, [384], BF16)
        P.op("sp", lambda e: e.dma_start(out=RB.ap[0:8, :], in_=rbT_d), writes=[RB.span()], dma="new")
        P.op("dve", lambda e: e.memset(Ft.ap[0:8, :], 0.0), writes=[Ft.span()])
        P.op("dve", lambda e: e.tensor_copy(out=Ft.ap[0:8, 127:143], in_=RB.ap[0:8, 0:16]),
             reads=[RB.span()], writes=[Ft.span()])
        for (v, lo, hi) in rel_bucket_ranges():
            P.op("dve", lambda e, v=v, lo=lo, hi=hi: e.tensor_scalar(
                out=Ft.ap[0:8, 127 + lo:127 + hi], in0=Ft.ap[0:8, 127 + lo:127 + hi],
                scalar1=RB.ap[0:8, v:v + 1], scalar2=None, op0=ALU.add),
                reads=[RB.span(), Ft.span()], writes=[Ft.span()])
        P.op("dve", lambda e: e.tensor_scalar(out=Ft.ap[0:8, 127:384], in0=Ft.ap[0:8, 127:384],
                                              scalar1=RB.ap[0:8, 31:32], scalar2=8.0,
                                              op0=ALU.subtract, op1=ALU.mult),
             reads=[RB.span(), Ft.span()], writes=[Ft.span()])
        P.op("dve", lambda e: e.memset(Ft.ap[0:8, 0:127], NEG), reads=[Ft.span()], writes=[Ft.span()])
        P.op("dve", lambda e: e.tensor_copy(out=Fb.ap[0:8, :], in_=Ft.ap[0:8, :]), reads=[Ft.span()],
             writes=[Fb.span()])
        fd_trk = Trk("Fd", 8 * 384 * 2)
        P.op("sp", lambda e: e.dma_start(out=Fd, in_=Fb.ap[0:8, :]), reads=[Fb.span()],
             writes=[(fd_trk, 0, 8 * 384 * 2)], dma="new")
        P.op("sp", lambda e: e.dma_start(out=BT.ap, in_=bass.AP(Fd.tensor, 0, [[1, 128], [384, 8], [1, 256]])),
             reads=[(fd_trk, 0, 8 * 384 * 2)], writes=[BT.span()], dma="new")
        P.op("sp", lambda e: e.dma_start(out=b31bc.ap, in_=bcast(rb_d, 31 * 8, 8)), writes=[b31bc.span()],
             dma="new")

        if dbg_stage == "c2":
            dump("BT", BT.ap, [128, 8, 256], BF16, BT.span())
            P.emit()
            return nc, P, dbg
        lamt = carve(7168, [256], F32)
        lprod = carve(8192, [2, 64], F32)
        P.op("sp", lambda e: e.dma_start(out=lamt.ap, in_=bcast(lam4_d, 0, 256)), writes=[lamt.span()], dma="new")
        for i in range(2):
            P.op("dve", lambda e, i=i: e.tensor_tensor(out=lprod.ap[:, i, :], in0=lamt.ap[:, 128 * i:128 * i + 64],
                                                       in1=lamt.ap[:, 128 * i + 64:128 * i + 128], op=ALU.mult),
                 reads=[lamt.span()], writes=[lprod.span(i)])
            P.op("dve", lambda e, i=i: e.reduce_sum(out=small.ap[:, i:i + 1], in_=lprod.ap[:, i, :], axis=AX.X),
                 reads=[lprod.span(i)], writes=[small.span((i, i + 1))])
        P.op("act", lambda e: e.activation(out=small.ap[:, 2:4], in_=small.ap[:, 0:2], func=AF.Exp),
             reads=[small.span((0, 2))], writes=[small.span((2, 4))])
        P.op("dve", lambda e: e.tensor_tensor(out=nlam.ap, in0=small.ap[:, 3:4], in1=small.ap[:, 2:3],
                                              op=ALU.subtract),
             reads=[small.span((2, 4))], writes=[nlam.span()])
        P.op("dve", lambda e: e.tensor_scalar(out=nlam.ap, in0=nlam.ap, scalar1=-LAM_INIT, scalar2=None,
                                              op0=ALU.add),
             reads=[nlam.span()], writes=[nlam.span()])
        P.op("sp", lambda e: e.dma_start(out=gsub.ap, in_=bcast(subg_d, 0, 128)), writes=[gsub.span()], dma="new")
        P.op("dve", lambda e: e.tensor_scalar(out=gsub.ap, in0=gsub.ap, scalar1=1.0 - LAM_INIT, scalar2=None,
                                              op0=ALU.mult),
             reads=[gsub.span()], writes=[gsub.span()])

        if dbg_stage == "c3":
            dump("nlam", nlam.ap, [128, 1], F32, nlam.span())
            dump("gsub", gsub.ap, [128, 128], F32, gsub.span())
            P.emit()
            return nc, P, dbg
        cTt = carve(8704, [8, NB], F32)
        cact = carve(8768, [8, NB], F32)
        cactb = carve(8832, [8, NB], BF16)
        badaT = carve(8896, [48], F32)
        crep = [carve(9216 + 2048 * b, [8, 128], BF16) for b in range(NB)]
        babc = [carve(13312 + 2048 * i, [512], F32) for i in range(2)]
        gtmp = carve(17408, [512], F32)
        P.op("sp", lambda e: e.dma_start(out=cTt.ap, in_=cT_d), writes=[cTt.span()], dma="new")
        P.op("sp", lambda e: e.dma_start(out=badaT.ap, in_=badaT_d), writes=[badaT.span()], dma="new")
        P.op("sp", lambda e: e.dma_start(out=n1gT.ap, in_=n1gT_d), writes=[n1gT.span()], dma="new")
        P.op("sp", lambda e: e.dma_start(out=n2gT.ap, in_=n2gT_d), writes=[n2gT.span()], dma="new")
        P.op("act", lambda e: e.activation(out=cact.ap, in_=cTt.ap, func=AF.Silu), reads=[cTt.span()],
             writes=[cact.span()])
        P.op("dve", lambda e: e.tensor_copy(out=cactb.ap, in_=cact.ap), reads=[cact.span()], writes=[cactb.span()])
        for b in range(NB):
            for kc in range(8):
                P.op("dve", lambda e, b=b, kc=kc: e.tensor_scalar(
                    out=crep[b].ap[:, kc, :], in0=ones_f.ap, scalar1=cact.ap[:, kc, b:b + 1], scalar2=None,
                    op0=ALU.mult),
                    reads=[ones_f.span(), cact.span()], writes=[crep[b].span(kc)])
        modps = V(banks[6].ap[:, 0:96].rearrange("p (c b) -> p c b", b=NB), banks[6].trk, 0, 4, [48, NB])
        gtrk = Trk("gsc", 2 * NB * D * 4)
        for wt in range(12):
            sl = load_w(wada_v[:, :, 512 * wt:512 * wt + 512], 8, 512)
            for cc in range(4):
                c = 4 * wt + cc
                for kc in range(8):
                    mm(modps.ap[:, c, :], sl.ap[:, kc, 128 * cc:128 * cc + 128], cactb.ap[:, kc, :],
                       kc == 0, kc == 7, [sl.span(), cactb.span()], [modps.span(c)])
            if False:
                which = 0 if wt < 6 else 1
                half = wt % 2
                bb = babc[half]
                P.op("sp", lambda e, bb=bb, wt=wt: e.dma_start(out=bb.ap, in_=bcast(bada_d, 512 * wt, 512)),
                     writes=[bb.span()], dma="bb%d" % half)
                for b in range(NB):
                    ps = next_bank()
                    for kc in range(8):
                        mm(ps.ap, crep[b].ap[:, kc, :], sl.ap[:, kc, :], kc == 0, kc == 7,
                           [sl.span(), crep[b].span(kc)], [ps.span()])
                    P.op("dve", lambda e, ps=ps, bb=bb: e.tensor_tensor(out=gtmp.ap, in0=ps.ap, in1=bb.ap,
                                                                        op=ALU.add),
                         reads=[ps.span(), bb.span()], writes=[gtmp.span()])
                    row = 2 * b + which
                    off = (row * D + 512 * half) * 4
                    P.op("sp", lambda e, row=row, half=half: e.dma_start(
                        out=gsc[row:row + 1, 512 * half:512 * half + 512], in_=gtmp.ap[0:1, :]),
                        reads=[gtmp.span()], writes=[(gtrk, off, off + 2048)], dma="gsc")
        for b in range(NB):
            P.op("dve", lambda e, b=b: e.tensor_tensor(out=modT.ap[:, :, b], in0=modps.ap[:, :, b], in1=badaT.ap,
                                                       op=ALU.add),
                 reads=[modps.span(), badaT.span()], writes=[modT.span()])
        for b in range(NB):
            P.op("dve", lambda e, b=b: e.scalar_tensor_tensor(out=gm1.ap[:, :, b], in0=modT.ap[:, 8:16, b],
                                                              scalar=1.0, in1=n1gT.ap, op0=ALU.add, op1=ALU.mult),
                 reads=[modT.span(), n1gT.span()], writes=[gm1.span()])
            P.op("dve", lambda e, b=b: e.scalar_tensor_tensor(out=gm2.ap[:, :, b], in0=modT.ap[:, 32:40, b],
                                                              scalar=1.0, in1=n2gT.ap, op0=ALU.add, op1=ALU.mult),
                 reads=[modT.span(), n2gT.span()], writes=[gm2.span()])

        if dbg_stage in ("c4", "c4a"):
            dump("modT", modT.ap, [128, 48, NB], F32, modT.span())
            dump("gm1", gm1.ap, [128, 8, NB], F32, gm1.span())
            P.emit()
            return nc, P, dbg
        def next_tp():
            return banks[7]

        def norm_to_T(src_ap_of_tb, src_span_of_tb, dstT, dst_t0, gm, sh_lo, b, scr_off):
            junk = carve(scr_off, [D], BF16)
            xs = carve(scr_off + 2048, [4, D], BF16)
            for tb in range(4):
                sap, ssp = src_ap_of_tb(tb), src_span_of_tb(tb)
                c0 = 8 + tb
                P.op("act", lambda e, sap=sap, c0=c0: e.activation(out=junk.ap, in_=sap, func=AF.Square,
                                                                   accum_out=small.ap[:, c0:c0 + 1]),
                     reads=[ssp], writes=[junk.span(), small.span((c0, c0 + 1))])
            rstd_from(small.ap[:, 8:12], small.span((8, 12)), small.ap[:, 12:16], small.span((12, 16)), D, RMS_EPS)
            for tb in range(4):
                sap, ssp = src_ap_of_tb(tb), src_span_of_tb(tb)
                c0 = 8 + tb
                P.op("dve", lambda e, sap=sap, tb=tb, c0=c0: e.tensor_scalar(
                    out=xs.ap[:, tb, :], in0=sap, scalar1=small.ap[:, c0 + 4:c0 + 5], scalar2=None, op0=ALU.mult),
                    reads=[ssp, small.span((c0 + 4, c0 + 5))], writes=[xs.span(tb)])
            if dbg_stage == "a1":
                dump("xs", xs.ap, [128, 4, D], BF16, xs.span())
                dump("small", small.ap, [128, 64], F32, small.span())
                raise StopIteration
            for kc in range(8):
                tp = next_bank()
                for tb in range(4):
                    mm(tp.ap[:, tb * 128:tb * 128 + 128], xs.ap[:, tb, kc * 128:kc * 128 + 128], ident.ap, True, True,
                       [xs.span(tb), ident.span()], [tp.span((tb * 128, tb * 128 + 128))])
                P.op("dve", lambda e, tp=tp, kc=kc: e.tensor_scalar(
                    out=dstT.ap[:, kc, dst_t0:dst_t0 + 512], in0=tp.ap, scalar1=gm.ap[:, kc, b:b + 1],
                    scalar2=modT.ap[:, sh_lo + kc, b:b + 1], op0=ALU.mult, op1=ALU.add),
                    reads=[tp.span(), gm.span(), modT.span()], writes=[dstT.span(kc, (dst_t0, dst_t0 + 512))])

        A_V1 = 0
        A_QK = 33280
        A_WQK = 49664
        A_PT = 57856
        A_FIN = 64000
        V1 = carve(A_V1, [16, 8, 129], BF16)
        QK = [carve(A_QK + 8192 * i, [2, S], BF16) for i in range(2)]
        WQK = [carve(A_WQK + 4096 * i, [2, 8, 128], BF16) for i in range(2)]
        PT = [[carve(A_PT + 2048 * i + 1024 * m, [512], BF16) for m in range(2)] for i in range(3)]
        FT1 = [carve(A_FIN + 512 * i, [128], F32) for i in range(2)]
        FO = [carve(A_FIN + 1024 + 512 * i, [128], F32) for i in range(4)]
        FJ = [carve(A_FIN + 3072 + 256 * i, [128], BF16) for i in range(2)]
        OAB = [carve(A_FIN + 3584 + 1024 * i, [4, 128], BF16) for i in range(2)]
        ST = [[banks[0], banks[1]], [banks[2], banks[3]]]
        ACCS = [carve(69632, [387], F32), carve(69632 + 1548, [387], F32), carve(69632 + 3096, [258], F32)]
        tokST = [Trk("tokST%d" % i, 4) for i in range(2)]
        tokPT = [Trk("tokPT%d" % i, 4) for i in range(3)]
        tokACC = Trk("tokACC", 4)
        for i in (0, 1, 2, 3):
            pass

        def acc(m, i):
            if i < 3:
                bk = banks[4 + m]
                return bk, bk.ap[:, 129 * i:129 * i + 129], (129 * i, 129 * i + 129)
            bk = banks[6]
            return bk, bk.ap[:, 129 * m:129 * m + 129], (129 * m, 129 * m + 129)

        pending = []

        def tick_pending():
            for it in pending:
                it[0] -= 1
            while pending and pending[0][0] <= 0:
                pending.pop(0)[1]()

        def flush_pending():
            while pending:
                pending.pop(0)[1]()

        def attention(b, h, s, inj=()):
            inj = list(inj)
            stepc = [0]
            QT = lambda m, a, bnd: QK[s].ap[64 * m:64 * m + 64, 0, a:bnd]
            KT = lambda m, a, bnd: QK[s].ap[64 * m:64 * m + 64, 1, a:bnd]
            for I in range(4):
                last = 4 * I + 3
                started = set()

                def qk(j):
                    r = j - 4 * I
                    c0 = 128 * max(r, 0)
                    has_bias = j >= 4 * I - 1
                    stb = ST[j % 2]
                    pt = PT[j % 3]
                    for m in range(2):
                        mm(stb[m].ap[:, c0:512], KT(m, j * 128, j * 128 + 128), QT(m, 512 * I + c0, 512 * I + 512),
                           True, not has_bias,
                           [QK[s].span(1, (j * 128, j * 128 + 128)), QK[s].span(0, (512 * I + c0, 512 * I + 512))],
                           [stb[0].span(), stb[1].span(), (tokST[j % 2], 0, 4)])
                    if has_bias:
                        if r < 0:
                            cb, boff, w = 0, 128, 128
                        else:
                            cb, boff, w = c0, 0, min(256, 512 - c0)
                        for m in range(2):
                            mm(stb[m].ap[:, cb:cb + w], Jb.ap, BT.ap[:, h, boff:boff + w], False, True,
                               [Jb.span(), BT.span(h)], [stb[0].span(), stb[1].span(), (tokST[j % 2], 0, 4)],
                               skip_group_check=True)
                    for m in range(2):
                        P.op("act", lambda e, m=m, stb=stb, pt=pt, c0=c0: e.activation(
                            out=pt[m].ap[:, c0:512], in_=stb[m].ap[:, c0:512], func=AF.Exp,
                            bias=b31bc.ap[:, h:h + 1], scale=0.125),
                            reads=[(tokST[j % 2], 0, 4), stb[0].span(), stb[1].span(), b31bc.span()],
                            writes=[pt[0].span(), pt[1].span(), (tokPT[j % 3], 0, 4)])

                def pv(j):
                    r = j - 4 * I
                    pt = PT[j % 3]
                    for i in range(max(r, 0), 4):
                        for m in range(2):
                            bk, aap, (a0, a1) = acc(m, i)
                            first = id(bk) not in started
                            started.add(id(bk))
                            mm(aap, pt[m].ap[:, 128 * i:128 * i + 128], V1.ap[:, j, h, :], first, j == 4 * I + i,
                               [(tokPT[j % 3], 0, 4), pt[0].span(), pt[1].span(), V1.span(j, h)],
                               [banks[4].span(), banks[5].span(), banks[6].span(), (tokACC, 0, 4)],
                               skip_group_check=True)

                qk(0)
                for j in range(last + 1):
                    if j + 1 <= last:
                        qk(j + 1)
                    stepc[0] += 1
                    if inj and stepc[0] % 2 == 0:
                        inj.pop(0)()
                    tick_pending()
                    pv(j)

                flush_pending()
                for k in range(3):
                    P.op("dve", lambda e, k=k: e.tensor_copy(out=ACCS[k].ap, in_=banks[4 + k].ap[:, 0:ACCS[k].shape[0]]),
                         reads=[(tokACC, 0, 4), banks[4 + k].span()], writes=[ACCS[k].span()])

                oab = OAB[I % 2]
                for i in range(4):
                    par = i % 2
                    if i < 3:
                        a0, a1 = ACCS[0].ap[:, 129 * i:129 * i + 129], ACCS[1].ap[:, 129 * i:129 * i + 129]
                        rd0, rd1 = ACCS[0].span((129 * i, 129 * i + 129)), ACCS[1].span((129 * i, 129 * i + 129))
                    else:
                        a0, a1 = ACCS[2].ap[:, 0:129], ACCS[2].ap[:, 129:258]
                        rd0, rd1 = ACCS[2].span((0, 129)), ACCS[2].span((129, 258))
                    cs = 24 + 3 * par
                    sm = lambda k, cs=cs: small.ap[:, cs + k:cs + k + 1]
                    sms = lambda k, cs=cs: small.span((cs + k, cs + k + 1))
                    P.op("dve", lambda e, a0=a0, sm=sm: e.reciprocal(out=sm(0), in_=a0[:, 128:129]),
                         reads=[rd0], writes=[sms(0)])
                    P.op("dve", lambda e, a1=a1, sm=sm: e.reciprocal(out=sm(1), in_=a1[:, 128:129]),
                         reads=[rd1], writes=[sms(1)])
                    P.op("dve", lambda e, sm=sm: e.tensor_tensor(out=sm(2), in0=sm(1), in1=nlam.ap, op=ALU.mult),
                         reads=[sms(1), nlam.span()], writes=[sms(2)])
                    t1, o = FT1[par], FO[i]
                    P.op("dve", lambda e, a0=a0, sm=sm, t1=t1: e.tensor_scalar(
                        out=t1.ap, in0=a0[:, 0:128], scalar1=sm(0), scalar2=None, op0=ALU.mult),
                        reads=[rd0, sms(0)], writes=[t1.span()])
                    P.op("dve", lambda e, a1=a1, sm=sm, t1=t1, o=o: e.scalar_tensor_tensor(
                        out=o.ap, in0=a1[:, 0:128], scalar=sm(2), in1=t1.ap, op0=ALU.mult, op1=ALU.add),
                        reads=[rd1, sms(2), t1.span()], writes=[o.span()])

                def part_b():
                    for i in range(4):
                        o, fj = FO[i], FJ[i % 2]
                        P.op("act", lambda e, o=o, fj=fj, i=i: e.activation(out=fj.ap, in_=o.ap, func=AF.Square,
                                                                           accum_out=small.ap[:, 16 + i:17 + i]),
                             reads=[o.span()], writes=[fj.span(), small.span((16 + i, 17 + i))])
                    rstd_from(small.ap[:, 16:20], small.span((16, 20)), small.ap[:, 20:24], small.span((20, 24)), 128,
                              RMS_EPS)

                def part_c(oab=oab, I=I):
                    for i in range(4):
                        o = FO[i]
                        P.op("dve", lambda e, o=o, i=i: e.scalar_tensor_tensor(
                            out=oab.ap[:, i, :], in0=o.ap, scalar=small.ap[:, 20 + i:21 + i], in1=gsub.ap,
                            op0=ALU.mult, op1=ALU.mult),
                            reads=[o.span(), small.span((20 + i, 21 + i)), gsub.span()], writes=[oab.span(i)])
                    tp = next_tp()
                    for i in range(4):
                        mm(tp.ap[:, i * 128:i * 128 + 128], oab.ap[:, i, :], ident.ap, True, True,
                           [oab.span(i), ident.span()], [tp.span()])
                    P.op("dve", lambda e: e.tensor_copy(out=OAT.ap[:, h, 512 * I:512 * I + 512], in_=tp.ap),
                         reads=[tp.span()], writes=[OAT.span(h, (512 * I, 512 * I + 512))])

                pending.append([4, part_b])
                pending.append([7, part_c])
            while inj:
                inj.pop(0)()

        def linear_fm2(specs, nchunks, consume):
            for g0 in range(0, nchunks, 4):
                n = min(4, nchunks - g0)
                sls = [load_w(wv[:, :, c0 + 128 * g0:c0 + 128 * (g0 + n)], 8, 128 * n) for (wv, c0, _, _) in specs]
                for cc in range(n):
                    pss = []
                    for sl, (wv, c0, rhs, t0) in zip(sls, specs):
                        ps = next_bank()
                        for kc in range(8):
                            mm(ps.ap, sl.ap[:, kc, 128 * cc:128 * cc + 128], rhs.ap[:, kc, t0:t0 + 512], kc == 0,
                               kc == 7, [sl.span(), rhs.span(kc, (t0, t0 + 512))], [ps.span()])
                        pss.append(ps)
                    consume(g0 + cc, *pss)

        def make_gbc(dst, ch0, b):
            DG = [carve(57344 + 512 * i, [128], F32) for i in range(2)]
            for half in range(2):
                ps = next_bank()
                for cc in range(4):
                    c = half * 4 + cc
                    dg = DG[c % 2]
                    P.op("dve", lambda e, dg=dg, c=c: e.tensor_scalar(
                        out=dg.ap, in0=ident_f.ap, scalar1=modT.ap[:, ch0 + c, b:b + 1], scalar2=None,
                        op0=ALU.mult),
                        reads=[ident_f.span(), modT.span()], writes=[dg.span()])
                    mm(ps.ap[:, cc * 128:cc * 128 + 128], ones_f.ap, dg.ap, True, True,
                       [ones_f.span(), dg.span()], [ps.span()])
                P.op("dve", lambda e, ps=ps, half=half: e.tensor_copy(out=dst.ap[:, half * 512:half * 512 + 512],
                                                                      in_=ps.ap),
                     reads=[ps.span()], writes=[dst.span((half * 512, half * 512 + 512))])

        P.op("sp", lambda e: e.dma_start(out=bc[0].ap, in_=bcast(lng_d, 0, D)), writes=[bc[0].span()], dma="new")
        P.op("sp", lambda e: e.dma_start(out=bc[1].ap, in_=bcast(lnb_d, 0, D)), writes=[bc[1].span()], dma="new")
        P.op("sp", lambda e: e.dma_start(out=bc[2].ap, in_=bcast(bsp_d, 0, D)), writes=[bc[2].span()], dma="new")
        P.op("sp", lambda e: e.dma_start(out=bc[5].ap, in_=bcast(gf_d, 0, D)), writes=[bc[5].span()], dma="new")

        def tail(b, pc):
            tok0 = pc * 512
            win_v, wpa_v, wpb_v, wout_v, wfi_v, wfo_v = win_bv, wpa_bv, wpb_bv, wout_bv, wfi_bv, wfo_bv
            w_reads[:] = [(cv_trk, 0, 64)]
            x1p = carve(0, [4, D], F32)
            S1 = carve(16384, [8, 512], BF16)
            S2 = carve(24576, [8, 512], BF16)
            S3 = carve(32768, [8, 512], BF16)
            vLN = carve(24576, [4, D], BF16)
            GZ = [carve(40960 + 4096 * i, [D], F32) for i in range(2)]
            SG = [carve(49152 + 2048 * i, [512], F32) for i in range(2)]
            TMP = [carve(53248 + 2048 * i, [512], F32) for i in range(2)]
            YB = [carve(57344 + 4096 * i, [D], F32) for i in range(2)]
            h2T = carve(26624, [8, 512], BF16)
            actT = carve(34816, [11, 512], BF16)
            cnt = {"sg": 0, "tmp": 0}

            def nsg():
                cnt["sg"] += 1
                return SG[cnt["sg"] % 2]

            def ntmp():
                cnt["tmp"] += 1
                return TMP[cnt["tmp"] % 2]

            def c_u(c, ps):
                P.op("act", lambda e: e.activation(out=S1.ap[:, c, :], in_=ps.ap, func=AF.Gelu_apprx_tanh),
                     reads=[ps.span()], writes=[S1.span(c)])
            linear_fm2([(win_v, 3072, hT, tok0)], 8, c_u)

            wg = [load_w(win_v[:, :, 4096 + 512 * i:4096 + 512 * i + 512], 8, 512) for i in range(2)]
            m1_slots = [rr["slot"] % NSLOT, (rr["slot"] + 1) % NSLOT]
            S4 = carve(65536, [8, 512], BF16)
            for tb in range(4):
                gz = GZ[tb % 2]
                t0 = tok0 + tb * 128
                for half in range(2):
                    ps = next_bank()
                    for kc in range(8):
                        mm(ps.ap, hT.ap[:, kc, t0:t0 + 128], wg[half].ap[:, kc, :], kc == 0, kc == 7,
                           [hT.span(kc, (t0, t0 + 128)), wg[half].span()], [ps.span()])
                    P.op("act", lambda e, ps=ps, gz=gz, half=half: e.activation(
                        out=gz.ap[:, half * 512:half * 512 + 512], in_=ps.ap, func=AF.Gelu_apprx_tanh),
                        reads=[ps.span()], writes=[gz.span((half * 512, half * 512 + 512))])
                for half in range(2):
                    P.op("dve", lambda e, gz=gz, half=half: e.bn_stats(
                        out=small.ap[:, 32 + 6 * half:38 + 6 * half], in_=gz.ap[:, half * 512:half * 512 + 512]),
                        reads=[gz.span((half * 512, half * 512 + 512))],
                        writes=[small.span((32 + 6 * half, 38 + 6 * half))])
                P.op("dve", lambda e: e.bn_aggr(out=small.ap[:, 44:46], in_=small.ap[:, 32:44]),
                     reads=[small.span((32, 44))], writes=[small.span((44, 46))])
                rstd_from(small.ap[:, 45:46], small.span((45, 46)), small.ap[:, 46:47], small.span((46, 47)), 1, LN_EPS)
                P.op("dve", lambda e: e.scalar_tensor_tensor(out=small.ap[:, 47:48], in0=small.ap[:, 44:45],
                                                             scalar=-1.0, in1=small.ap[:, 46:47], op0=ALU.mult,
                                                             op1=ALU.mult),
                     reads=[small.span((44, 47))], writes=[small.span((47, 48))])
                P.op("act", lambda e, gz=gz: e.activation(out=gz.ap, in_=gz.ap, func=AF.Identity,
                                                          bias=small.ap[:, 47:48], scale=small.ap[:, 46:47]),
                     reads=[gz.span(), small.span((46, 48))], writes=[gz.span()])
                P.op("dve", lambda e, gz=gz: e.tensor_tensor(out=gz.ap, in0=gz.ap, in1=bc[0].ap, op=ALU.mult),
                     reads=[gz.span(), bc[0].span()], writes=[gz.span()])
                P.op("dve", lambda e, gz=gz, tb=tb: e.tensor_tensor(out=vLN.ap[:, tb, :], in0=gz.ap, in1=bc[1].ap,
                                                                    op=ALU.add),
                     reads=[gz.span(), bc[1].span()], writes=[vLN.span(tb)])
                slm = wslot[m1_slots[tb % 2]]
                ca = 256 * tb
                P.op("pool", lambda e, slm=slm, ca=ca: e.dma_start(out=slm.ap[:, :, 0:256], in_=wpa_v[:, :, ca:ca + 256]),
                     reads=list(w_reads), writes=[slm.span()], dma="w%d" % m1_slots[tb % 2])
                P.op("pool", lambda e, slm=slm, ca=ca: e.dma_start(out=slm.ap[:, :, 256:512],
                                                                   in_=win_v[:, :, 5120 + ca:5120 + ca + 256]),
                     reads=list(w_reads), writes=[slm.span()], dma="w%d" % m1_slots[tb % 2])
                for cc in range(2):
                    c = 2 * tb + cc
                    ps_a, ps_g = next_bank(), next_bank()
                    for kc in range(8):
                        mm(ps_a.ap, slm.ap[:, kc, 128 * cc:128 * cc + 128], OAT.ap[:, kc, tok0:tok0 + 512], kc == 0,
                           kc == 7, [slm.span(), OAT.span(kc, (tok0, tok0 + 512))], [ps_a.span()])
                    for kc in range(8):
                        mm(ps_g.ap, slm.ap[:, kc, 256 + 128 * cc:256 + 128 * cc + 128], hT.ap[:, kc, tok0:tok0 + 512],
                           kc == 0, kc == 7, [slm.span(), hT.span(kc, (tok0, tok0 + 512))], [ps_g.span()])
                    sg = nsg()
                    P.op("act", lambda e, sg=sg, ps_g=ps_g: e.activation(out=sg.ap, in_=ps_g.ap, func=AF.Sigmoid),
                         reads=[ps_g.span()], writes=[sg.span()])
                    P.op("dve", lambda e, sg=sg, ps_a=ps_a, c=c: e.tensor_tensor(out=S4.ap[:, c, :], in0=sg.ap,
                                                                                 in1=ps_a.ap, op=ALU.mult),
                         reads=[sg.span(), ps_a.span()], writes=[S4.span(c)])
            rr["slot"] += 2

            for g in range(8):
                ps = next_bank()
                for tb in range(4):
                    mm(ps.ap[:, tb * 128:tb * 128 + 128], vLN.ap[:, tb, g * 128:g * 128 + 128], wsT.ap[:, g, :],
                       True, True, [vLN.span(tb), wsT.span(g)], [ps.span()])
                tmp = ntmp()
                for tb in range(4):
                    P.op("dve", lambda e, ps=ps, tmp=tmp, tb=tb, g=g: e.tensor_tensor(
                        out=tmp.ap[:, tb * 128:tb * 128 + 128], in0=ps.ap[:, tb * 128:tb * 128 + 128],
                        in1=bc[2].ap[:, g * 128:g * 128 + 128], op=ALU.add),
                        reads=[ps.span(), bc[2].span()],
                        writes=[tmp.span((tb * 128, tb * 128 + 128))])
                P.op("dve", lambda e, tmp=tmp, g=g: e.tensor_tensor(out=S3.ap[:, g, :], in0=tmp.ap, in1=S1.ap[:, g, :],
                                                                    op=ALU.mult),
                     reads=[tmp.span(), S1.span(g)], writes=[S3.span(g)])

            def c_b(c, ps_b, ps_g):
                sg, tmp = nsg(), ntmp()
                P.op("act", lambda e: e.activation(out=sg.ap, in_=ps_g.ap, func=AF.Sigmoid),
                     reads=[ps_g.span()], writes=[sg.span()])
                P.op("dve", lambda e: e.tensor_tensor(out=tmp.ap, in0=sg.ap, in1=ps_b.ap, op=ALU.mult),
                     reads=[sg.span(), ps_b.span()], writes=[tmp.span()])
                P.op("dve", lambda e: e.tensor_tensor(out=S2.ap[:, c, :], in0=tmp.ap, in1=S4.ap[:, c, :], op=ALU.add),
                     reads=[tmp.span(), S4.span(c)], writes=[S2.span(c)])
            linear_fm2([(wpb_v, 0, S3, 0), (win_v, 6144, hT, tok0)], 8, c_b)

            wo = [load_w(wout_v[:, :, 512 * i:512 * i + 512], 8, 512) for i in range(2)]
            for tb in range(4):
                xb = GZ[tb % 2]
                t0 = tok0 + tb * 128
                P.op("sp", lambda e, xb=xb, t0=t0: e.dma_start(out=xb.ap, in_=x_d[b, t0:t0 + 128, :]),
                     writes=[xb.span()], dma="gz%d" % (tb % 2))
                for dh in range(2):
                    ps = next_bank()
                    for kc in range(8):
                        mm(ps.ap, S2.ap[:, kc, tb * 128:tb * 128 + 128], wo[dh].ap[:, kc, :], kc == 0, kc == 7,
                           [S2.span(kc, (tb * 128, tb * 128 + 128)), wo[dh].span()], [ps.span()])
                    tmp = ntmp()
                    P.op("dve", lambda e, ps=ps, tmp=tmp, dh=dh: e.tensor_tensor(
                        out=tmp.ap, in0=ps.ap, in1=bc[3].ap[:, dh * 512:dh * 512 + 512], op=ALU.mult),
                        reads=[ps.span(), bc[3].span()], writes=[tmp.span()])
                    P.op("dve", lambda e, tmp=tmp, xb=xb, tb=tb, dh=dh: e.tensor_tensor(
                        out=x1p.ap[:, tb, dh * 512:dh * 512 + 512], in0=tmp.ap, in1=xb.ap[:, dh * 512:dh * 512 + 512],
                        op=ALU.add),
                        reads=[tmp.span(), xb.span((dh * 512, dh * 512 + 512))],
                        writes=[x1p.span(tb, (dh * 512, dh * 512 + 512))])

            if dbg_stage == "x1" and b == 0 and pc == 0:
                dump("x1", x1p.ap, [128, 4, D], F32, x1p.span())
                return

            if dbg_stage == "e_g":
                dump("bc0", bc[4].ap, [128, D], F32, bc[4].span())
                dump("bc1", bc[5].ap, [128, D], F32, bc[5].span())
                raise StopIteration
            norm_to_T(lambda tb: x1p.ap[:, tb, :], lambda tb: x1p.span(tb), h2T, 0, gm2, 24, b, 16384)
            if dbg_stage == "e_h2T":
                dump("h2T", h2T.ap, [128, 8, 512], BF16, h2T.span())
                dump("bc0", bc[4].ap, [128, D], F32, bc[4].span())
                raise StopIteration
            for hh in range(2):
                for g0 in (0, 4, 8):
                    n = min(4, 11 - g0)
                    cg = hh * 1408 + 128 * g0
                    slg = load_w(wfi_v[:, :, cg:cg + 128 * n], 8, 128 * n)
                    slu = load_w(wfi_v[:, :, FF + cg:FF + cg + 128 * n], 8, 128 * n)
                    for cc in range(n):
                        psg, psu = next_bank(), next_bank()
                        for ps, sl in ((psg, slg), (psu, slu)):
                            for kc in range(8):
                                mm(ps.ap, sl.ap[:, kc, 128 * cc:128 * cc + 128], h2T.ap[:, kc, :], kc == 0, kc == 7,
                                   [sl.span(), h2T.span(kc)], [ps.span()])
                        sg = nsg()
                        P.op("act", lambda e, sg=sg, psg=psg: e.activation(out=sg.ap, in_=psg.ap, func=AF.Silu),
                             reads=[psg.span()], writes=[sg.span()])
                        c = g0 + cc
                        P.op("dve", lambda e, sg=sg, psu=psu, c=c: e.tensor_tensor(out=actT.ap[:, c, :], in0=sg.ap,
                                                                                   in1=psu.ap, op=ALU.mult),
                             reads=[sg.span(), psu.span()], writes=[actT.span(c)])
                for dh in range(2):
                    wa = load_w(wfo_v[:, hh * 11:hh * 11 + 8, dh * 512:dh * 512 + 512], 8, 512)
                    wb = load_w(wfo_v[:, hh * 11 + 8:hh * 11 + 11, dh * 512:dh * 512 + 512], 3, 512)
                    for tb in range(4):
                        ps = next_bank()
                        for kc in range(11):
                            w_ap = wa.ap[:, kc, :] if kc < 8 else wb.ap[:, kc - 8, :]
                            w_sp = wa.span() if kc < 8 else wb.span()
                            mm(ps.ap, actT.ap[:, kc, tb * 128:tb * 128 + 128], w_ap, kc == 0, kc == 10,
                               [actT.span(kc, (tb * 128, tb * 128 + 128)), w_sp], [ps.span()])
                        tmp = ntmp()
                        P.op("dve", lambda e, ps=ps, tmp=tmp, dh=dh: e.tensor_tensor(
                            out=tmp.ap, in0=ps.ap, in1=bc[4].ap[:, dh * 512:dh * 512 + 512], op=ALU.mult),
                            reads=[ps.span(), bc[4].span()], writes=[tmp.span()])
                        xs_ = x1p.span(tb, (dh * 512, dh * 512 + 512))
                        P.op("dve", lambda e, tmp=tmp, tb=tb, dh=dh: e.tensor_tensor(
                            out=x1p.ap[:, tb, dh * 512:dh * 512 + 512], in0=x1p.ap[:, tb, dh * 512:dh * 512 + 512],
                            in1=tmp.ap, op=ALU.add),
                            reads=[tmp.span(), xs_], writes=[xs_])
            if dbg_stage == "e_ffn":
                dump("x2", x1p.ap, [128, 4, D], F32, x1p.span())
                raise StopIteration
            fj = carve(16384, [D], BF16)
            for tb in range(4):
                P.op("act", lambda e, tb=tb: e.activation(out=fj.ap, in_=x1p.ap[:, tb, :], func=AF.Square,
                                                          accum_out=small.ap[:, 48 + tb:49 + tb]),
                     reads=[x1p.span(tb)], writes=[fj.span(), small.span((48 + tb, 49 + tb))])
            rstd_from(small.ap[:, 48:52], small.span((48, 52)), small.ap[:, 52:56], small.span((52, 56)), D, RMS_EPS)
            for tb in range(4):
                par = tb % 2
                yb = YB[par]
                t0 = tok0 + tb * 128
                P.op("dve", lambda e, tb=tb, yb=yb: e.scalar_tensor_tensor(
                    out=yb.ap, in0=x1p.ap[:, tb, :], scalar=small.ap[:, 52 + tb:53 + tb], in1=bc[5].ap, op0=ALU.mult,
                    op1=ALU.mult),
                    reads=[x1p.span(tb), small.span((52 + tb, 53 + tb)), bc[5].span()], writes=[yb.span()])
                P.op("sp", lambda e, yb=yb, t0=t0: e.dma_start(out=y_d[b, t0:t0 + 128, :], in_=yb.ap),
                     reads=[yb.span()], dma="yb%d" % par, final=True)

        def run_all():
            for b in range(NB):
                w_reads[:] = []
                XB = [carve(36864 + 4096 * i, [D], F32) for i in range(4)]
                for tg in range(4):
                    for tb in range(4):
                        t = tg * 4 + tb
                        xb = XB[t % 4]
                        P.op("sp", lambda e, xb=xb, t=t, b=b: e.dma_start(out=xb.ap, in_=x_d[b, t * 128:(t + 1) * 128, :]),
                             writes=[xb.span()], dma="xb%d" % (t % 4))
                    norm_to_T(lambda tb: XB[tb].ap, lambda tb: XB[tb].span(),
                              hT, tg * 512, gm1, 0, b, 53248)
                    if dbg_stage == "a2":
                        for kc in range(8):
                            dump("hT%d" % kc, hT.ap[:, kc, 0:512], [128, 512], BF16, hT.span(kc, (0, 512)))
                        raise StopIteration
                if dbg_stage == "a3":
                    for kc in range(8):
                        dump("hT%d" % kc, hT.ap[:, kc, 1536:2048], [128, 512], BF16, hT.span(kc, (1536, 2048)))
                    raise StopIteration
                if dbg_stage == "hT":
                    dump("hT", hT.ap, [128, 8, S], BF16, hT.span())
                    dump("modT", modT.ap, [128, 48, NB], F32, modT.span())
                    dump("BT", BT.ap, [128, 8, 256], BF16, BT.span())
                    dump("wsT", wsT.ap, [128, 8, 128], BF16, wsT.span())
                    dump("nlam", nlam.ap, [128, 1], F32, nlam.span())
                    return

                P.op("dve", lambda e: e.memset(V1.ap[:, :, :, 128:129], 1.0), writes=[V1.span()])
                wv = [load_w(win_v[:, :, 2048 + 512 * i:2048 + 512 * i + 512], 8, 512) for i in range(2)]
                for tb in range(16):
                    for half in range(2):
                        ps = next_bank()
                        for kc in range(8):
                            mm(ps.ap, hT.ap[:, kc, tb * 128:tb * 128 + 128], wv[half].ap[:, kc, :], kc == 0, kc == 7,
                               [hT.span(kc, (tb * 128, tb * 128 + 128)), wv[half].span()], [ps.span()])
                        dst = V1.ap[:, tb, 4 * half:4 * half + 4, 0:128]
                        src = ps.ap.rearrange("p (h e) -> p h e", e=128)
                        if (tb + half) % 2 == 0:
                            P.op("act", lambda e, dst=dst, src=src: e.copy(out=dst, in_=src), reads=[ps.span()],
                                 writes=[V1.span(tb, (4 * half, 4 * half + 4))])
                        else:
                            P.op("dve", lambda e, dst=dst, src=src: e.tensor_copy(out=dst, in_=src), reads=[ps.span()],
                                 writes=[V1.span(tb, (4 * half, 4 * half + 4))])

                def load_wqk(h):
                    s = h % 2
                    for which in range(2):
                        c0 = which * 1024 + h * 128
                        P.op("pool", lambda e, s=s, which=which, c0=c0: e.dma_start(
                            out=WQK[s].ap[:, which, :, :], in_=win_v[:, :, c0:c0 + 128]),
                            writes=[WQK[s].span(which)], dma="wqk%d%d" % (s, which))

                def qk_groups(h):
                    s = h % 2
                    gs = []
                    for which in range(2):
                        for t8 in range(8):
                            def g(which=which, t8=t8, s=s):
                                ps = banks[7]
                                a, bnd = t8 * 256, t8 * 256 + 256
                                for kc in range(8):
                                    mm(ps.ap[:, 0:256], WQK[s].ap[:, which, kc, :], hT.ap[:, kc, a:bnd],
                                       kc == 0, kc == 7,
                                       [WQK[s].span(which), hT.span(kc, (a, bnd))], [ps.span()])
                                dst = QK[s].ap[:, which, a:bnd]
                                P.op("dve", lambda e, dst=dst, ps=ps: e.tensor_copy(out=dst, in_=ps.ap[:, 0:256]),
                                     reads=[ps.span()], writes=[QK[s].span(which, (a, bnd))])
                            gs.append(g)
                    return gs

                load_wqk(0)
                for g in qk_groups(0):
                    g()
                for h in range(H):
                    s = h % 2
                    inj = []
                    if h + 1 < H:
                        load_wqk(h + 1)
                        inj = qk_groups(h + 1)
                    if dbg_stage == "qk":
                        for q4 in range(4):
                            dump("QT%d" % q4, QK[0].ap[:, 0, q4 * 512:q4 * 512 + 512], [128, 512], BF16,
                                 QK[0].span(0, (q4 * 512, q4 * 512 + 512)))
                        for tb in range(2):
                            dump("V%d" % tb, V1.ap[:, tb, :, :], [128, 8, 129], BF16, V1.span(tb))
                        return
                    if b == 0:
                        issue_cv(5)
                    attention(b, h, s, inj)
                    if dbg_stage == "att1":
                        flush_pending()
                        for q4 in range(4):
                            dump("OAT%d" % q4, OAT.ap[:, 0, q4 * 512:q4 * 512 + 512], [128, 512], BF16,
                                 OAT.span(0, (q4 * 512, q4 * 512 + 512)))
                            dump("QT%d" % q4, QK[0].ap[:, 0, q4 * 512:q4 * 512 + 512], [128, 512], BF16,
                                 QK[0].span(0, (q4 * 512, q4 * 512 + 512)))
                        return
                flush_pending()
                if dbg_stage == "att":
                    dump("OAT", OAT.ap, [128, 8, S], BF16, OAT.span())
                    return

                issue_cv(64)
                make_gbc(bc[3], 16, b)
                make_gbc(bc[4], 40, b)
                for pc in range(4):
                    tail(b, pc)
                    if dbg_stage == "x1":
                        return
                    if dbg_stage == "p0":
                        return
                    if dbg_stage == "p1" and pc == 1:
                        return
                if dbg_stage == "b0":
                    return

        try:
            run_all()
        except StopIteration:
            pass
        P.emit()
    return nc, P, dbg


def make_in_maps(inp):
    f = lambda a: np.ascontiguousarray(np.asarray(a, dtype=np.float32))
    x, c = f(inp["x"]), f(inp["c"])
    shared = {
        "w_ada": f(inp["w_ada"][0]),
        "b_ada": f(inp["b_ada"][0]).reshape(1, 6 * D),
        "b_adaT": f(f(inp["b_ada"][0]).reshape(48, 128).T),
        "n1gT": f(f(inp["norm1_g"][0]).reshape(8, 128).T),
        "n2gT": f(f(inp["norm2_g"][0]).reshape(8, 128).T),
        "w_in": f(inp["w_in"][0]),
        "lam4": f(np.stack([f(inp["lambda_q1"][0]), f(inp["lambda_k1"][0]), f(inp["lambda_q2"][0]),
                            f(inp["lambda_k2"][0])])).reshape(1, 256),
        "subln_g": f(inp["subln_g"][0]).reshape(1, 128),
        "ln_v_g": f(inp["ln_v_g"][0]).reshape(1, D),
        "ln_v_b": f(inp["ln_v_b"][0]).reshape(1, D),
        "w_spT": f(f(inp["w_spatial"][0]).transpose(0, 2, 1)),
        "b_sp": f(inp["b_spatial"][0]).reshape(1, D),
        "w_proj_a": f(inp["w_proj_a"][0]),
        "w_proj_b": f(inp["w_proj_b"][0]),
        "w_out": f(inp["w_out"][0]),
        "w_ffn_in": f(inp["w_ffn_in"][0]),
        "w_ffn_out": f(inp["w_ffn_out"][0]),
        "rel_biasT": f(f(inp["rel_bias"]).T),
        "rel_bias": f(inp["rel_bias"]),
        "final_g": f(inp["final_g"]).reshape(1, D),
    }
    maps = []
    for i in range(NCORES):
        m = dict(shared)
        m["x"] = f(x[NB * i:NB * i + NB])
        m["cT"] = f(c[NB * i:NB * i + NB].reshape(NB, 8, 128).transpose(2, 1, 0))
        maps.append(m)
    return maps


_CACHE = {}


def kernel(**inputs):
    if "nc" not in _CACHE:
        _CACHE["nc"] = build()[0]
    nc = _CACHE["nc"]
    maps = make_in_maps(inputs)
    res = run_bass_kernel_spmd(nc, maps, core_ids=list(range(NCORES)))
    return np.concatenate([np.asarray(r["y"], dtype=np.float32) for r in res.results], axis=0)
```
